# Optimizing a Trainium2 kernel written in Bass

```python
import math
import jax, jax.numpy as jnp
from jax import lax
import numpy as np

D_MODEL = 1024
BATCH = 16
SEQ = 256
DEPTH = 1
DEC_BATCH = 8
DEC_SEQ = 2048
PAST_LEN = 512

GRID_W = 64
EPS = 1e-6
A_HEADS = 4
A_HEAD_DIM = 128
A_WIDTH = A_HEADS * A_HEAD_DIM
CHUNK = 128
MLA_HEADS = 4
QK_NOPE = 128
QK_ROPE = 64
V_HEAD = 128
B_WIDTH = MLA_HEADS * V_HEAD
Q_RANK = 256
KV_RANK = 128
AXIS_PAIRS = QK_ROPE // 4
ROPE_THETA = 10000.0
Q_BLOCK = 128
ATTN_SCALE = 1.0 / math.sqrt(QK_NOPE + QK_ROPE)
IN_WIDTHS = (A_WIDTH, A_WIDTH, A_WIDTH, Q_RANK, KV_RANK, QK_ROPE, B_WIDTH)
IN_WIDTH = 3 * A_WIDTH + Q_RANK + KV_RANK + QK_ROPE + B_WIDTH
MIX_WIDTH = A_WIDTH + B_WIDTH

kernel_name = 'hymba_chunkmlp_mla_diffusion_step'


def _rmsnorm(x, g):
    xf = x.astype(jnp.float32)
    y = xf * lax.rsqrt(jnp.mean(xf * xf, axis=-1, keepdims=True) + EPS)
    return (y * g.astype(jnp.float32)).astype(x.dtype)


def _split_in(z):
    idx = [int(i) for i in np.cumsum(IN_WIDTHS)[:-1]]
    return jnp.split(z, idx, axis=-1)


def _grid_rope(n_tokens, dtype):
    rows = n_tokens // GRID_W
    r, cc = jnp.meshgrid(jnp.arange(rows), jnp.arange(GRID_W), indexing='ij')
    r = r.reshape(-1).astype(jnp.float32)
    cc = cc.reshape(-1).astype(jnp.float32)
    inv = ROPE_THETA ** (-jnp.arange(AXIS_PAIRS, dtype=jnp.float32) / AXIS_PAIRS)
    ang = jnp.concatenate([r[:, None] * inv, cc[:, None] * inv], axis=-1)
    return jnp.cos(ang).astype(dtype), jnp.sin(ang).astype(dtype)


def _rope(x, cos, sin):
    half = QK_ROPE // 2
    x1, x2 = x[..., :half], x[..., half:]
    return jnp.concatenate([x1 * cos - x2 * sin, x1 * sin + x2 * cos], axis=-1)


def _chunk_mlp(u, v, w_s, b_s, g_v):
    B, L, _ = u.shape
    n = L // CHUNK
    u = jax.nn.gelu(u)
    v = _rmsnorm(jax.nn.gelu(v).reshape(B, L, A_HEADS, A_HEAD_DIM), g_v)
    v = v.reshape(B, n, CHUNK, A_HEADS, A_HEAD_DIM)
    mixed = jnp.einsum('hpq,bnqhd->bnphd', w_s, v) + b_s.T[:, :, None]
    return u * mixed.reshape(B, L, A_WIDTH)


def _expand_kv(ckv, w_ukv):
    B, L, _ = ckv.shape
    kv = (ckv @ w_ukv).reshape(B, L, MLA_HEADS, QK_NOPE + V_HEAD)
    return kv[..., :QK_NOPE], kv[..., QK_NOPE:]


def _block_attention(q_nope, q_ropes, key_sets):
    B, Lq, H, _ = q_nope.shape
    nb = Lq // Q_BLOCK
    v_all = jnp.concatenate([ks[2] for ks in key_sets], axis=1)

    def one_block(i):
        s0 = i * Q_BLOCK
        qn = lax.dynamic_slice_in_dim(q_nope, s0, Q_BLOCK, axis=1)
        logits = []
        for qr, (kn, kr, _) in zip(q_ropes, key_sets):
            qrb = lax.dynamic_slice_in_dim(qr, s0, Q_BLOCK, axis=1)
            logits.append(jnp.einsum('bqhn,bkhn->bhqk', qn, kn)
                          + jnp.einsum('bqhr,bkr->bhqk', qrb, kr))
        s = jnp.concatenate(logits, axis=-1).astype(jnp.float32) * ATTN_SCALE
        p = jax.nn.softmax(s, axis=-1).astype(v_all.dtype)
        return jnp.einsum('bhqk,bkhv->bqhv', p, v_all)

    out = lax.map(one_block, jnp.arange(nb))
    return jnp.moveaxis(out, 0, 1).reshape(B, Lq, H * V_HEAD)


def _front(x, cond, norm_g, w_ada, b_ada, w_in, w_s, b_s, g_v, q_norm_g, w_uq, kv_norm_g):
    B, L, _ = x.shape
    mod = jax.nn.silu(cond) @ w_ada + b_ada
    shift, scale, gate = jnp.split(mod[:, None, :], 3, axis=-1)
    h = _rmsnorm(x, norm_g) * (1 + scale) + shift
    u, v, g_a, c_q, c_kv, k_rope, g_b = _split_in(h @ w_in)
    a_out = _chunk_mlp(u, v, w_s, b_s, g_v) * jax.nn.silu(g_a)
    q = (_rmsnorm(c_q, q_norm_g) @ w_uq).reshape(B, L, MLA_HEADS, QK_NOPE + QK_ROPE)
    ckv = _rmsnorm(c_kv, kv_norm_g)
    return gate, a_out, g_b, q[..., :QK_NOPE], q[..., QK_NOPE:], ckv, k_rope


def _back(x, gate, a_out, attn, g_b, w_o):
    y = jnp.concatenate([a_out, attn * jax.nn.silu(g_b)], axis=-1) @ w_o
    return x + gate * y


def setup_inputs(seed: int = 0) -> dict:
    key = jax.random.key(seed)
    ks = jax.random.split(key, 20)

    def nrm(k, shape, scale=1.0):
        return jax.random.normal(k, shape, jnp.float32) * scale

    return {
        'x_prompt': nrm(ks[0], (BATCH, SEQ, D_MODEL)),
        'x_sample': nrm(ks[1], (DEC_BATCH, DEC_SEQ, D_MODEL)),
        'cache_ckv': nrm(ks[2], (DEC_BATCH, DEPTH, PAST_LEN, KV_RANK)),
        'cache_krope': nrm(ks[3], (DEC_BATCH, DEPTH, PAST_LEN, QK_ROPE)),
        'c': nrm(ks[4], (DEC_BATCH, D_MODEL)),
        'c_ctx': nrm(ks[5], (D_MODEL,)),
        'norm_g': 1.0 + nrm(ks[6], (DEPTH, D_MODEL), 0.02),
        'w_ada': nrm(ks[7], (DEPTH, D_MODEL, 3 * D_MODEL), D_MODEL ** -0.5),
        'b_ada': nrm(ks[8], (DEPTH, 3 * D_MODEL), 0.01),
        'w_in': nrm(ks[9], (DEPTH, D_MODEL, IN_WIDTH), D_MODEL ** -0.5),
        'w_s': nrm(ks[10], (DEPTH, A_HEADS, CHUNK, CHUNK), CHUNK ** -0.5),
        'b_s': nrm(ks[11], (DEPTH, A_HEADS, CHUNK), 0.02),
        'g_v': 1.0 + nrm(ks[12], (DEPTH, A_HEADS, A_HEAD_DIM), 0.02),
        'q_norm_g': 1.0 + nrm(ks[13], (DEPTH, Q_RANK), 0.02),
        'w_uq': nrm(ks[14], (DEPTH, Q_RANK, MLA_HEADS * (QK_NOPE + QK_ROPE)), Q_RANK ** -0.5),
        'kv_norm_g': 1.0 + nrm(ks[15], (DEPTH, KV_RANK), 0.02),
        'w_ukv': nrm(ks[16], (DEPTH, KV_RANK, MLA_HEADS * (QK_NOPE + V_HEAD)), KV_RANK ** -0.5),
        'w_o': nrm(ks[17], (DEPTH, MIX_WIDTH, D_MODEL), MIX_WIDTH ** -0.5),
        'final_g': 1.0 + nrm(ks[18], (D_MODEL,), 0.02),
    }


def reference(x_prompt, x_sample, cache_ckv, cache_krope, c, c_ctx, norm_g, w_ada, b_ada,
              w_in, w_s, b_s, g_v, q_norm_g, w_uq, kv_norm_g, w_ukv, w_o, final_g):
    xp, xs = x_prompt, x_sample
    cos, sin = _grid_rope(x_sample.shape[1], x_sample.dtype)
    new_ckv, new_kr = [], []
    for l in range(DEPTH):
        p = (norm_g[l], w_ada[l], b_ada[l], w_in[l], w_s[l], b_s[l], g_v[l],
             q_norm_g[l], w_uq[l], kv_norm_g[l])
        gate, a_out, g_b, qn, qr, ckv, kr = _front(xp, c_ctx[None, :], *p)
        kn, vv = _expand_kv(ckv, w_ukv[l])
        attn = _block_attention(qn, (qr,), ((kn, kr, vv),))
        xp = _back(xp, gate, a_out, attn, g_b, w_o[l])
        new_ckv.append(ckv)
        new_kr.append(kr)
        gate, a_out, g_b, qn, qr, ckv, kr = _front(xs, c, *p)
        kn, vv = _expand_kv(ckv, w_ukv[l])
        ckv_c = cache_ckv[:, l]
        kn_c, v_c = _expand_kv(ckv_c, w_ukv[l])
        kr_c = cache_krope[:, l]
        qr_rot = _rope(qr, cos[:, None, :], sin[:, None, :])
        kr_rot = _rope(kr, cos, sin)
        attn = _block_attention(qn, (qr_rot, qr), ((kn, kr_rot, vv), (kn_c, kr_c, v_c)))
        xs = _back(xs, gate, a_out, attn, g_b, w_o[l])
    y_prompt = _rmsnorm(xp, final_g)
    y_sample = _rmsnorm(xs, final_g)
    return (y_prompt, y_sample, jnp.stack(new_ckv, axis=1), jnp.stack(new_kr, axis=1))
```

```python
import contextlib
import math
import numpy as np
import ml_dtypes
import concourse.bass as bass
import concourse.mybir as mybir
from concourse.bass_utils import run_bass_kernel_spmd

F32 = mybir.dt.float32
BF16 = mybir.dt.bfloat16
AF = mybir.ActivationFunctionType
ALU = mybir.AluOpType
AX = mybir.AxisListType

NCORES = 8
D = 1024
TS = 2048
TP = 256
NP_PER_CORE = 2
PAST = 512
QB = 256
EPS = 1e-6
ATTN_SCALE = 1.0 / math.sqrt(192.0)
WIN = 2560
WUQ = 1280


class Tok:
    __slots__ = ("eng", "sem", "val")

    def __init__(self, eng, sem, val):
        self.eng, self.sem, self.val = eng, sem, val


class Buf:
    def __init__(self, name, excl=False):
        self.name = name
        self.excl = excl
        self.w = None
        self.r = []
        self.dsem = None
        self.dcount = 0


class Prog:
    ENGS = ("pe", "act", "dve", "pool", "sp")

    def __init__(self, nc, stack):
        self.nc = nc
        self.stack = stack
        self.q = {e: [] for e in self.ENGS}
        self.esem = {e: stack.enter_context(nc.semaphore("es_" + e)) for e in self.ENGS}
        self.ecount = {e: 0 for e in self.ENGS}
        self.waited = {e: {} for e in self.ENGS}
        self.pending = {e: None for e in self.ENGS}
        self.out_toks = []
        self.cap = None

    def replay_item(self, it):
        if it["kind"] == "op":
            self.op(it["eng"], it["fn"], it["reads"], it["writes"], it["signal"])
        else:
            self.dma(it["eng"], it["out"], it["in_"], it["reads"], it["writes"], it["is_output"])

    def sbuf(self, name, shape, dt):
        return self.stack.enter_context(self.nc.sbuf_tensor("s_" + name, list(shape), dt))

    def psum(self, name, shape, dt):
        return self.stack.enter_context(self.nc.psum_tensor("p_" + name, list(shape), dt))

    def newsem(self, name):
        return self.stack.enter_context(self.nc.semaphore(name))

    def _wait(self, eng, tok):
        assert tok.val is not None, "dependency on unsignalled op (%s)" % tok.eng
        key = id(tok.sem)
        if self.waited[eng].get(key, 0) >= tok.val:
            return
        self.waited[eng][key] = tok.val
        self.q[eng].append(("wait", tok.sem, tok.val))

    def _deps(self, eng, reads, writes, dma_sem=None):
        need = {}

        def add(t):
            assert t.val is not None, "dependency on unsignalled op (%s)" % t.eng
            k = id(t.sem)
            if k not in need or need[k].val < t.val:
                need[k] = t

        for b in reads:
            if b.w is not None:
                add(b.w)
            if b.excl:
                for t in b.r:
                    if t.eng != eng:
                        add(t)
        for b in writes:
            if b.w is not None and b.w.eng != eng and not (dma_sem is not None and b.w.sem is dma_sem):
                add(b.w)
            for t in b.r:
                if t.eng != eng:
                    add(t)
        for t in need.values():
            self._wait(eng, t)

    def begin(self):
        self.cap = []

    def end(self):
        c, self.cap = self.cap, None
        return c

    def op(self, eng, fn, reads=(), writes=(), signal=True, cost=None):
        if self.cap is not None:
            self.cap.append(dict(kind="op", eng=eng, fn=fn, reads=tuple(reads), writes=tuple(writes),
                                 signal=signal, cost=cost))
            return None
        self._deps(eng, reads, writes)
        if signal:
            self.ecount[eng] += 1
            v = self.ecount[eng]
            p = self.pending[eng]
            if p is not None:
                p.val = v
                self.pending[eng] = None
            tok = Tok(eng, self.esem[eng], v)
        else:
            if self.pending[eng] is None:
                self.pending[eng] = Tok(eng, self.esem[eng], None)
            tok = self.pending[eng]
        for b in reads:
            b.r.append(tok)
        for b in writes:
            b.w = tok
            b.r = []
        self.q[eng].append(("op", fn, self.esem[eng] if signal else None, 1))
        return tok

    def dma(self, eng, out, in_, reads=(), writes=(), is_output=False, cost=None):
        if self.cap is not None:
            self.cap.append(dict(kind="dma", eng=eng, out=out, in_=in_, reads=tuple(reads), writes=tuple(writes),
                                 is_output=is_output, signal=True, cost=cost))
            return None
        sb = (list(writes) + list(reads))[0]
        if sb.dsem is None:
            sb.dsem = self.newsem("ds_" + sb.name)
        self._deps(eng, reads, writes, dma_sem=(sb.dsem if writes else None))
        sb.dcount += 16
        tok = Tok(None, sb.dsem, sb.dcount)
        for b in reads:
            b.r.append(tok)
        for b in writes:
            b.w = tok
            b.r = []
        if is_output:
            self.out_toks.append(tok)
        self.q[eng].append(("op", lambda e: e.dma_start(out=out, in_=in_), sb.dsem, 16))
        return tok

    def finish(self, eng="sp"):
        last = {}
        for t in self.out_toks:
            k = id(t.sem)
            if k not in last or last[k].val < t.val:
                last[k] = t
        for t in last.values():
            self.q[eng].append(("wait", t.sem, t.val))

    def emit(self):
        with self.nc.Block() as block:
            def replay(name):
                def f(e):
                    for it in self.q[name]:
                        if it[0] == "wait":
                            e.wait_ge(it[1], it[2])
                        else:
                            ins = it[1](e)
                            if it[2] is not None:
                                ins.then_inc(it[2], it[3])
                return f
            block.tensor(replay("pe"))
            block.scalar(replay("act"))
            block.vector(replay("dve"))
            block.gpsimd(replay("pool"))
            block.sync(replay("sp"))


class Ring:
    def __init__(self, P, name, n, shape, dt, psum=False):
        self.items = []
        self.held = set()
        self.name = name
        for i in range(n):
            nm = "%s%d" % (name, i)
            t = P.psum(nm, shape, dt) if psum else P.sbuf(nm, shape, dt)
            self.items.append((t, Buf(nm, excl=psum)))
        self.i = 0

    def next(self, hold=False):
        n = len(self.items)
        for _ in range(n):
            idx = self.i % n
            self.i += 1
            if idx not in self.held:
                if hold:
                    self.held.add(idx)
                return self.items[idx]
        raise RuntimeError("ring %s exhausted" % self.name)

    def release(self, item):
        for idx, it in enumerate(self.items):
            if it[0] is item[0]:
                self.held.discard(idx)
                return


DEF_COST = {"pe": 0.16, "act": 0.75, "dve": 0.75, "pool": 1.1, "sp": 2.5}


def schedule(items, slack_pe=0.0, slack_other=0.0):
    groups = []
    cur = None
    for it in items:
        if it["eng"] == "pe" and it["kind"] == "op":
            if cur is None:
                cur = []
                groups.append(cur)
            cur.append(it)
            if it["signal"]:
                cur = None
        else:
            groups.append([it])
    w_end, r_end, eng_free, w_dma = {}, {}, {}, {}
    out = []
    for gi, g in enumerate(groups):
        start = 0.0
        cost = 0.0
        for it in g:
            for b in it["reads"]:
                start = max(start, w_end.get(id(b), 0.0))
                if b.excl:
                    start = max(start, r_end.get(id(b), 0.0))
            for b in it["writes"]:
                if not (it["kind"] == "dma" and w_dma.get(id(b), False)):
                    start = max(start, w_end.get(id(b), 0.0))
                start = max(start, r_end.get(id(b), 0.0))
            c = it["cost"]
            if c is None:
                c = 2.5 if it["kind"] == "dma" else DEF_COST[it["eng"]]
            cost += c
        end = start + cost + (slack_pe if (g[0]["kind"] == "op" and g[0]["eng"] == "pe") else slack_other)
        for it in g:
            for b in it["reads"]:
                r_end[id(b)] = max(r_end.get(id(b), 0.0), end)
            for b in it["writes"]:
                w_end[id(b)] = max(end, w_end.get(id(b), 0.0)) if it["kind"] == "dma" else end
                w_dma[id(b)] = (it["kind"] == "dma")
                r_end[id(b)] = 0.0
        out.append((start, gi, g))
    out.sort(key=lambda t: (t[0], t[1]))
    return [(t[0], t[2]) for t in out]


def build_program():
    nc = bass.Bass("TRN2", target_bir_lowering=False)

    def din(name, shape):
        return nc.dram_tensor(name, list(shape), F32, kind="ExternalInput").ap()

    def dout(name, shape):
        return nc.dram_tensor(name, list(shape), F32, kind="ExternalOutput").ap()

    xs_d = din("xs", [TS, D])
    xp_d = din("xp", [NP_PER_CORE * TP, D])
    cckv_d = din("cckv", [PAST, 128])
    ckr_d = din("ckr", [PAST, 64])
    cvT_d = din("cvT", [128, 16])
    w_ada_d = din("w_ada", [D, 3 * D])
    b_adaT_d = din("b_adaT", [128, 16])
    b_gate_d = din("b_gate", [1, D])
    norm_gT_d = din("norm_gT", [128, 8])
    w_in_d = din("w_in", [D, 2496])
    w_sT_d = din("w_sT", [128, 512])
    b_s_d = din("b_s", [1, 512])
    g_v_d = din("g_v", [1, 512])
    q_norm_gT_d = din("q_norm_gT", [128, 2])
    w_uq_d = din("w_uq", [256, 768])
    kv_norm_g_d = din("kv_norm_g", [1, 128])
    w_ukv_d = din("w_ukv", [128, 1024])
    w_o_d = din("w_o", [D, D])
    final_g_d = din("final_g", [1, D])
    cos_d = din("cosT", [64, TS])
    sin_d = din("sinT", [64, TS])

    ys_d = dout("ys", [TS, D])
    yp_d = dout("yp", [NP_PER_CORE * TP, D])
    nckv_d = dout("nckv", [NP_PER_CORE * TP, 128])
    nkr_d = dout("nkr", [NP_PER_CORE * TP, 64])

    NKEY = TS + PAST + NP_PER_CORE * TP

    with contextlib.ExitStack() as st:
        P = Prog(nc, st)

        def persist(name, shape, dt):
            return P.sbuf(name, shape, dt), Buf(name)

        ident_b, b_ident_b = persist("ident_b", [128, 128], BF16)
        ones_f, b_ones_f = persist("ones_f", [128, 128], F32)
        ones_b, b_ones_b = persist("ones_b", [128, 128], BF16)
        cv_f, b_cv_f = persist("cv_f", [128, 16], F32)
        sc_f, b_sc_f = persist("sc_f", [128, 16], F32)
        sc_b, b_sc_b = persist("sc_b", [128, 16], BF16)
        b_adaT, b_b_adaT = persist("b_adaT", [128, 16], F32)
        norm_gT, b_norm_gT = persist("norm_gT", [128, 8], F32)
        tmp16, b_tmp16 = persist("tmp16", [128, 16], F32)
        g1T, b_g1T = persist("g1T", [128, 16], F32)
        shT, b_shT = persist("shT", [128, 16], F32)
        gate_bc = [persist("gate_bc%d" % j, [128, D], F32) for j in range(2)]
        final_g_bc, b_final_g_bc = persist("final_g_bc", [128, D], F32)
        g_v_bc, b_g_v_bc = persist("g_v_bc", [128, 512], F32)
        kvg_bc, b_kvg_bc = persist("kvg_bc", [128, 128], F32)
        qngT, b_qngT = persist("qngT", [128, 2], F32)
        cosT, b_cosT = persist("cosT_sb", [64, TS], BF16)
        sinT, b_sinT = persist("sinT_sb", [64, TS], BF16)
        w_in_b, b_w_in = persist("w_in_b", [128, 8 * WIN], BF16)
        w_o_b, b_w_o = persist("w_o_b", [128, 8 * D], BF16)
        w_uq_b, b_w_uq = persist("w_uq_b", [128, 2 * WUQ], BF16)
        w_nT_b, b_w_nT = persist("w_nT_b", [128, 512], BF16)
        w_v_b, b_w_v = persist("w_v_b", [128, 512], BF16)
        w_sT_b, b_w_sT = persist("w_sT_b", [128, 512], BF16)
        b_s_b, b_b_s = persist("b_s_b", [1, 512], BF16)
        ckvT = P.sbuf("ckvT", [128, NKEY], BF16)
        ckv_tok = P.sbuf("ckv_tok", [128, NKEY], BF16)
        KrT = P.sbuf("KrT", [128, NKEY], BF16)
        b_kblk = [Buf("kblk%d" % i) for i in range(NKEY // QB)]
        b_kcache = Buf("kcache")
        QpT = [persist("QpT%d" % i, [128, 4 * QB], BF16) for i in range(2)]
        QrT = [persist("QrT%d" % i, [128, 4 * QB], BF16) for i in range(2)]
        a_outT = [persist("a_outT%d" % i, [128, 4 * QB], BF16) for i in range(2)]
        sgbT = [persist("sgbT%d" % i, [128, 4 * QB], BF16) for i in range(2)]
        mixB = [persist("mixB%d" % i, [128, 4 * QB], BF16) for i in range(2)]
        hT = P.sbuf("hT", [128, 8 * QB], BF16)
        b_hT2 = [[Buf("hT0d"), Buf("hT0a")], [Buf("hT1d"), Buf("hT1a")]]
        b_hTh = b_hT2[0] + b_hT2[1]
        ug = P.sbuf("ug", [128, 8 * QB], F32)
        b_t_f, b_gvv = Buf("t_f"), Buf("gvv")
        t_f = ug[:, 0:4 * QB]
        gs, b_gs = persist("gs", [128, 8 * QB], F32)
        cq_sb, b_cq_sb = persist("cq_sb", [128, 2 * QB], F32)
        rstd_bc, b_rstd_bc = persist("rstd_bc", [128, QB], F32)
        cqnT, b_cqnT = persist("cqnT", [128, 2 * QB], BF16)
        o_sb, b_o_sb = persist("o_sb", [128, 4 * QB], BF16)
        acc = [persist("acc%d" % i, [128, 512], F32) for i in range(2)]

        xin = Ring(P, "xin", 4, [128, D], F32)
        xn = Ring(P, "xn", 2, [128, D], BF16)
        yres = Ring(P, "yres", 2, [128, D], F32)
        vn_r = Ring(P, "vn", 2, [128, 512], BF16)
        qn_r = Ring(P, "qn", 2, [128, 512], BF16)
        tmpa_r = Ring(P, "tmpa", 2, [128, 512], F32)
        tmpb_r = Ring(P, "tmpb", 2, [128, 512], F32)
        PT_r = Ring(P, "PT", 8, [128, 512], BF16)
        cko_r = Ring(P, "cko", 2, [128, 192], F32)
        st_r = Ring(P, "st", 8, [128, 4], F32)
        st2_r = Ring(P, "st2", 8, [128, 4], F32)
        st3_r = Ring(P, "st3", 8, [128, 4], F32)

        psF = Ring(P, "psF", 2, [128, 512], F32, psum=True)
        psS = Ring(P, "psS", 4, [128, 512], F32, psum=True)

        def psT_next():
            t, bb = psF.next()
            return t[:, :].bitcast(BF16), bb
        psO = [(P.psum("psO%d" % i, [128, 512], F32), Buf("psO%d" % i, excl=True)) for i in range(2)]

        idf, b_idf = persist("ident_f", [128, 128], F32)
        P.op("pool", lambda e: e.memset(idf[:, 0:128], 0.0), writes=[b_idf])
        P.op("pool", lambda e: e.affine_select(out=idf[:, 0:128], in_=idf[:, 0:128], pattern=[[-1, 128]],
                                               compare_op=ALU.not_equal, fill=1.0, base=0,
                                               channel_multiplier=1),
             reads=[b_idf], writes=[b_idf])
        P.op("dve", lambda e: e.tensor_copy(out=ident_b[:], in_=idf[:, 0:128]), reads=[b_idf], writes=[b_ident_b])
        P.op("dve", lambda e: e.memset(ones_f[:], 1.0), writes=[b_ones_f])
        P.op("dve", lambda e: e.memset(ones_b[:], 1.0), writes=[b_ones_b])
        b_krt_init = Buf("krt_init")
        P.op("dve", lambda e: e.memset(KrT[:, :], 0.0), writes=[b_krt_init] + b_kblk + [b_kcache])

        P.dma("sp", cv_f[:], cvT_d, writes=[b_cv_f])
        P.dma("sp", b_adaT[:], b_adaT_d, writes=[b_b_adaT])
        P.dma("sp", norm_gT[:], norm_gT_d, writes=[b_norm_gT])
        P.op("act", lambda e: e.activation(out=sc_f[:], in_=cv_f[:], func=AF.Silu), reads=[b_cv_f], writes=[b_sc_f])
        P.op("dve", lambda e: e.tensor_copy(out=sc_b[:], in_=sc_f[:]), reads=[b_sc_f], writes=[b_sc_b])
        for j in range(2):
            for kc in range(8):
                col = kc * 2 + j
                o = (j * 8 + kc) * 128
                P.op("dve", lambda e, o=o, col=col: e.tensor_scalar_mul(
                    out=hT[:, o:o + 128], in0=ones_b[:, :], scalar1=sc_f[:, col:col + 1]),
                    reads=[b_ones_b, b_sc_f], writes=b_hTh)

        wslots = [xn.items[0], xn.items[1], QpT[0], QpT[1], QrT[0], QrT[1], a_outT[0], a_outT[1],
                  sgbT[0], sgbT[1], mixB[0], mixB[1], (o_sb, b_o_sb)]
        wpieces = []
        pieces = [(kc, piece) for kc in range(8) for piece in range(3)]

        def issue_wada(i):
            kc, piece = pieces[i]
            wb, b_wb = wslots[i % len(wslots)]
            P.dma("pool", wb[:, :], w_ada_d[kc * 128:(kc + 1) * 128, piece * 1024:(piece + 1) * 1024],
                  writes=[b_wb])
            wpieces.append((wb, b_wb))

        for i in range(len(wslots)):
            issue_wada(i)
        b_w_in_k = Buf("w_in_k")
        for kc in range(8):
            P.dma("pool", w_in_b[:, kc * WIN + 1792:kc * WIN + 1984], w_in_d[kc * 128:(kc + 1) * 128, 1792:1984],
                  writes=[b_w_in_k])
        P.dma("pool", cosT[:, :], cos_d, writes=[b_cosT])
        P.dma("pool", sinT[:, :], sin_d, writes=[b_sinT])

        pm, b_pm = psF.next()
        pg = [psS.items[0], psS.items[1], psS.items[2], psO[0]]
        first_pm = [True]
        for i, (kc, piece) in enumerate(pieces):
            wb, b_wb = wpieces[i]
            if piece < 2:
                for c in range(8):
                    cc = piece * 8 + c
                    fs = first_pm[0]
                    first_pm[0] = False
                    P.op("pe", lambda e, cc=cc, c=c, kc=kc, wb=wb, fs=fs: e.matmul(
                        pm[:, cc * 2:cc * 2 + 2], lhsT=wb[:, c * 128:(c + 1) * 128],
                        rhs=sc_b[:, kc * 2:kc * 2 + 2], start=fs, stop=(kc == 7),
                        skip_group_check=True),
                        reads=[b_wb, b_sc_b], writes=[b_pm], signal=(c == 7))
            else:
                for j in range(2):
                    for n in range(2):
                        pt, b_pt = pg[j * 2 + n]
                        o = (j * 8 + kc) * 128
                        P.op("pe", lambda e, pt=pt, o=o, n=n, kc=kc, wb=wb: e.matmul(
                            pt[:, :], lhsT=hT[:, o:o + 128], rhs=wb[:, n * 512:(n + 1) * 512],
                            start=(kc == 0), stop=(kc == 7)),
                            reads=[b_wb] + b_hTh, writes=[b_pt], signal=(j == 1 and n == 1))
            nxt = i + len(wslots)
            if nxt < len(pieces):
                issue_wada(nxt)
        for j in range(2):
            pmv = pm[:, 0:32].rearrange("p (c j) -> p c j", j=2)
            P.op("dve", lambda e, j=j, pmv=pmv: e.scalar_tensor_tensor(
                out=tmp16[:, j * 8:(j + 1) * 8], in0=pmv[:, 8:16, j], scalar=1.0, in1=b_adaT[:, 8:16],
                op0=ALU.add, op1=ALU.add), reads=[b_pm, b_b_adaT], writes=[b_tmp16])
            P.op("dve", lambda e, j=j: e.tensor_tensor(
                out=g1T[:, j * 8:(j + 1) * 8], in0=tmp16[:, j * 8:(j + 1) * 8], in1=norm_gT[:, 0:8],
                op=ALU.mult), reads=[b_tmp16, b_norm_gT], writes=[b_g1T])
            P.op("dve", lambda e, j=j, pmv=pmv: e.tensor_tensor(
                out=shT[:, j * 8:(j + 1) * 8], in0=pmv[:, 0:8, j], in1=b_adaT[:, 0:8],
                op=ALU.add), reads=[b_pm, b_b_adaT], writes=[b_shT])
        bg, b_bg = yres.next()
        P.dma("sp", bg[:], b_gate_d.partition_broadcast(128), writes=[b_bg])
        for j in range(2):
            for n in range(2):
                pt, b_pt = pg[j * 2 + n]
                g, b_g = gate_bc[j]
                P.op("dve", lambda e, pt=pt, g=g, n=n: e.tensor_tensor(
                    out=g[:, n * 512:(n + 1) * 512], in0=pt[:, :], in1=bg[:, n * 512:(n + 1) * 512],
                    op=ALU.add), reads=[b_pt, b_bg], writes=[b_g])

        P.dma("sp", kvg_bc[:], kv_norm_g_d.partition_broadcast(128), writes=[b_kvg_bc])
        P.dma("sp", qngT[:], q_norm_gT_d, writes=[b_qngT])
        P.dma("sp", g_v_bc[:], g_v_d.partition_broadcast(128), writes=[b_g_v_bc])
        P.dma("sp", final_g_bc[:], final_g_d.partition_broadcast(128), writes=[b_final_g_bc])

        def late_setup():
            for kc in range(8):
                P.dma("pool", w_in_b[:, kc * WIN:kc * WIN + 1792], w_in_d[kc * 128:(kc + 1) * 128, 0:1792],
                      reads=[b_w_in_k, b_cosT, b_sinT], writes=[b_w_in], cost=70.0)
                P.dma("pool", w_in_b[:, kc * WIN + 1984:kc * WIN + 2496], w_in_d[kc * 128:(kc + 1) * 128, 1984:2496],
                      reads=[b_w_in_k, b_cosT, b_sinT], writes=[b_w_in], cost=70.0)
            P.dma("pool", b_s_b[:], b_s_d, reads=[b_w_in], writes=[b_b_s], cost=5.0)
            P.dma("pool", w_sT_b[:, :], w_sT_d, reads=[b_w_in], writes=[b_w_sT], cost=5.0)
            for kc in range(8):
                o = kc * WIN
                P.op("dve", lambda e, o=o: e.tensor_scalar_mul(
                    out=w_in_b[:, o + 2496:o + 2528], in0=w_in_b[:, o + 1952:o + 1984], scalar1=-1.0),
                    reads=[b_w_in_k], writes=[b_w_in_k])
                P.op("dve", lambda e, o=o: e.tensor_copy(
                    out=w_in_b[:, o + 2528:o + 2560], in_=w_in_b[:, o + 1920:o + 1952]),
                    reads=[b_w_in_k], writes=[b_w_in_k])
            wn, b_wn = vn_r.next()
            ukv3 = w_ukv_d.rearrange("r (h c) -> r h c", c=256)
            P.dma("pool", wn[:, :].rearrange("p (h c) -> p h c", c=128), ukv3[:, :, 0:128], reads=[b_w_in], writes=[b_wn], cost=45.0)
            P.dma("pool", w_v_b[:, :].rearrange("p (h c) -> p h c", c=128), ukv3[:, :, 128:256], reads=[b_w_in], writes=[b_w_v], cost=45.0)
            psT, b_psT = psT_next()
            for h in range(4):
                P.op("pe", lambda e, h=h: e.transpose(psT[:, h * 128:(h + 1) * 128], wn[:, h * 128:(h + 1) * 128],
                                                      ident_b[:]),
                     reads=[b_wn, b_ident_b], writes=[b_psT], signal=(h == 3))
            P.op("dve", lambda e: e.tensor_copy(out=w_nT_b[:], in_=psT[:, 0:512]), reads=[b_psT], writes=[b_w_nT])
            for kc2 in range(2):
                stg, b_stg = xin.next()
                P.dma("sp", stg[:, 0:768], w_uq_d[kc2 * 128:(kc2 + 1) * 128, :], writes=[b_stg], cost=30.0)
                sv = stg[:, 0:768].rearrange("p (h c) -> p h c", c=192)
                base = kc2 * WUQ
                nope = w_uq_b[:, base:base + 512].rearrange("p (h c) -> p h c", c=128)
                rdup = w_uq_b[:, base + 512:base + 1024].rearrange("p (h c) -> p h c", c=128)
                rsw = w_uq_b[:, base + 1024:base + 1280].rearrange("p (h c) -> p h c", c=64)
                S = ATTN_SCALE
                P.op("dve", lambda e, nope=nope, sv=sv: e.tensor_scalar_mul(out=nope, in0=sv[:, :, 0:128], scalar1=S),
                     reads=[b_stg], writes=[b_w_uq])
                P.op("dve", lambda e, rdup=rdup, sv=sv: e.tensor_scalar_mul(out=rdup[:, :, 0:64], in0=sv[:, :, 128:192], scalar1=S),
                     reads=[b_stg], writes=[b_w_uq])
                P.op("dve", lambda e, rdup=rdup, sv=sv: e.tensor_scalar_mul(out=rdup[:, :, 64:128], in0=sv[:, :, 128:192], scalar1=S),
                     reads=[b_stg], writes=[b_w_uq])
                P.op("dve", lambda e, rsw=rsw, sv=sv: e.tensor_scalar_mul(out=rsw[:, :, 0:32], in0=sv[:, :, 160:192], scalar1=-S),
                     reads=[b_stg], writes=[b_w_uq])
                P.op("dve", lambda e, rsw=rsw, sv=sv: e.tensor_scalar_mul(out=rsw[:, :, 32:64], in0=sv[:, :, 128:160], scalar1=S),
                     reads=[b_stg], writes=[b_w_uq])
            P.dma("pool", ckv_tok[:, 16 * 128:20 * 128].rearrange("p (c r) -> p c r", r=128),
                  cckv_d.rearrange("(c p) r -> p c r", p=128), reads=[b_w_in], writes=[b_kcache], cost=45.0)
            psT1, b_psT1 = psT_next()
            for c in range(4):
                P.op("pe", lambda e, c=c: e.transpose(
                    psT1[:, c * 128:(c + 1) * 128], ckv_tok[:, (16 + c) * 128:(17 + c) * 128], ident_b[:]),
                    reads=[b_kcache, b_ident_b], writes=[b_psT1], signal=(c == 3))
            P.op("dve", lambda e: e.tensor_copy(out=ckvT[:, TS:TS + PAST], in_=psT1[:, 0:512]),
                 reads=[b_psT1], writes=[b_kcache])
            kb, b_kb = qn_r.next()
            P.op("dve", lambda e: e.memset(kb[:, :], 0.0), writes=[b_kb])
            P.dma("pool", kb[:, :].rearrange("p (c r) -> p c r", r=128)[:, :, 64:128],
                  ckr_d.rearrange("(c p) r -> p c r", p=128), reads=[b_w_in], writes=[b_kb], cost=45.0)
            psT2, b_psT2 = psT_next()
            for c in range(4):
                P.op("pe", lambda e, c=c: e.transpose(
                    psT2[:, c * 128:(c + 1) * 128], kb[:, c * 128:(c + 1) * 128], ident_b[:]),
                    reads=[b_kb, b_ident_b], writes=[b_psT2], signal=(c == 3))
            P.op("dve", lambda e: e.tensor_copy(out=KrT[64:128, TS:TS + PAST], in_=psT2[64:128, 0:512]),
                 reads=[b_psT2], writes=[b_kcache])
            for kc in range(8):
                P.dma("pool", w_o_b[:, kc * D:(kc + 1) * D], w_o_d[kc * 128:(kc + 1) * 128, :], reads=[b_w_in], writes=[b_w_o], cost=30.0)


        def rstd_from_sum(ss, b_ss, ncol, n_feat):
            l, b_l = st2_r.next()
            r, b_r = st3_r.next()
            P.op("act", lambda e: e.activation(out=l[:, 0:ncol], in_=ss, func=AF.Ln, bias=EPS, scale=1.0 / n_feat),
                 reads=[b_ss], writes=[b_l])
            P.op("act", lambda e: e.activation(out=r[:, 0:ncol], in_=l[:, 0:ncol], func=AF.Exp, scale=-0.5),
                 reads=[b_l], writes=[b_r])
            return r, b_r

        def norm_tile(xt, b_xt):
            xb, b_xb = xn.next()
            ss, b_ss = st_r.next()
            P.op("act", lambda e: e.activation(out=xb[:], in_=xt[:], func=AF.Square, accum_out=ss[:, 0:1]),
                 reads=[b_xt], writes=[b_xb, b_ss])
            r, b_r = rstd_from_sum(ss[:, 0:1], b_ss, 1, float(D))
            P.op("dve", lambda e: e.tensor_scalar_mul(out=xb[:], in0=xt[:], scalar1=r[:, 0:1]),
                 reads=[b_xt, b_r], writes=[b_xb])
            return xb, b_xb

        def transpose_tile(xb, b_xb, j, cond, pst=None, dst=None):
            psT, b_psT = pst if pst is not None else psT_next()
            if dst is not None:
                dt_, db_ = dst
                for kc in range(8):
                    P.op("pe", lambda e, kc=kc: e.transpose(
                        psT[:, kc * 128:(kc + 1) * 128], xb[:, kc * 128:(kc + 1) * 128], ident_b[:]),
                        reads=[b_xb, b_ident_b], writes=[b_psT], signal=(kc == 7))
                for kc in range(8):
                    col = cond * 8 + kc
                    if kc < 4:
                        P.op("dve", lambda e, kc=kc, col=col: e.tensor_scalar(
                            out=dt_[:, kc * 128:(kc + 1) * 128], in0=psT[:, kc * 128:(kc + 1) * 128],
                            scalar1=g1T[:, col:col + 1], scalar2=shT[:, col:col + 1],
                            op0=ALU.mult, op1=ALU.add), reads=[b_psT, b_g1T, b_shT], writes=[db_], cost=0.4)
                    else:
                        P.op("act", lambda e, kc=kc, col=col: e.activation(
                            out=dt_[:, kc * 128:(kc + 1) * 128], in_=psT[:, kc * 128:(kc + 1) * 128],
                            func=AF.Identity, bias=shT[:, col:col + 1], scale=g1T[:, col:col + 1]),
                            reads=[b_psT, b_g1T, b_shT], writes=[db_], cost=0.5)
                return
            for kc in range(8):
                P.op("pe", lambda e, kc=kc: e.transpose(
                    psT[:, kc * 128:(kc + 1) * 128], xb[:, kc * 128:(kc + 1) * 128], ident_b[:]),
                    reads=[b_xb, b_ident_b], writes=[b_psT], signal=(kc == 7))
            for kc in range(8):
                o = kc * QB + j * 128
                col = cond * 8 + kc
                if True:
                    P.op("dve", lambda e, kc=kc, o=o, col=col: e.tensor_scalar(
                        out=hT[:, o:o + 128], in0=psT[:, kc * 128:(kc + 1) * 128],
                        scalar1=g1T[:, col:col + 1], scalar2=shT[:, col:col + 1],
                        op0=ALU.mult, op1=ALU.add), reads=[b_psT, b_g1T, b_shT], writes=[b_hT2[j][0]])
                else:
                    P.op("act", lambda e, kc=kc, o=o, col=col: e.activation(
                        out=hT[:, o:o + 128], in_=psT[:, kc * 128:(kc + 1) * 128], func=AF.Identity,
                        bias=shT[:, col:col + 1], scale=g1T[:, col:col + 1]),
                        reads=[b_psT, b_g1T, b_shT], writes=[b_hT2[j][1]])

        def zfm(ps, b_ps, pcol, col0, ncols, last_signal=True, extra=()):
            for kc in range(8):
                P.op("pe", lambda e, kc=kc: e.matmul(
                    ps[0:ncols, pcol:pcol + QB], lhsT=w_in_b[:, kc * WIN + col0:kc * WIN + col0 + ncols],
                    rhs=hT[:, kc * QB:(kc + 1) * QB], start=(kc == 0), stop=(kc == 7)),
                    reads=[b_w_in] + b_hTh + list(extra), writes=[b_ps], signal=(last_signal and kc == 7))

        class RingOf:
            def __init__(self, items):
                self.items, self.i = list(items), 0

            def next(self):
                it = self.items[self.i % len(self.items)]
                self.i += 1
                return it

        kps = RingOf(list(psS.items) + list(psO))
        khT = RingOf([QpT[1], QrT[1], a_outT[1], sgbT[1], mixB[1]])

        def ktile_stages(seq, t):
            kb0 = seq["kblk0"]
            tok0 = kb0 * QB + t * 128
            chunk = tok0 // 128
            b_k = b_kblk[tok0 // QB]
            j = t % 2
            cond = seq["cond"]
            S = {}

            def s0():
                xt, b_xt = xin.next()
                row = seq["row0"] + t * 128
                P.dma("sp", xt[:], seq["x"][row:row + 128, :], writes=[b_xt])
                S["x"] = (xt, b_xt)

            def s1():
                S["xb"] = norm_tile(*S["x"])

            def s2():
                kt, b_kt = kps.next()
                S["h"] = khT.next()
                transpose_tile(S["xb"][0], S["xb"][1], j, cond, pst=(kt[:, :].bitcast(BF16), b_kt), dst=S["h"])

            def s3():
                ps, b_ps = kps.next()
                hk, b_hk = S["h"]
                for kc in range(8):
                    o = kc * 128
                    P.op("pe", lambda e, kc=kc, o=o: e.matmul(
                        ps[:, 0:192], lhsT=hk[:, o:o + 128], rhs=w_in_b[:, kc * WIN + 1792:kc * WIN + 1984],
                        start=(kc == 0), stop=(kc == 7)), reads=[b_w_in_k, b_hk], writes=[b_ps], signal=False)
                ngrp = 2 if seq["rope"] else 1
                for gi in range(ngrp):
                    col0 = 1920 if gi == 0 else 2496
                    for kc in range(8):
                        o = kc * 128
                        P.op("pe", lambda e, kc=kc, o=o, gi=gi, col0=col0: e.matmul(
                            ps[0:64, 256 + gi * 128:256 + (gi + 1) * 128],
                            lhsT=w_in_b[:, kc * WIN + col0:kc * WIN + col0 + 64], rhs=hk[:, o:o + 128],
                            start=(kc == 0), stop=(kc == 7)), reads=[b_w_in_k, b_hk], writes=[b_ps],
                            signal=(gi == ngrp - 1 and kc == 7))
                ck, b_ck = cko_r.next()
                S["ck"] = (ck, b_ck)
                ss, b_ss = st_r.next()
                jk, b_jk = tmpa_r.next()
                P.op("act", lambda e: e.activation(out=jk[:, 0:128], in_=ps[:, 0:128], func=AF.Square,
                                                   accum_out=ss[:, 0:1]), reads=[b_ps], writes=[b_jk, b_ss])
                r, b_r = rstd_from_sum(ss[:, 0:1], b_ss, 1, 128.0)
                P.op("dve", lambda e: e.scalar_tensor_tensor(
                    out=ck[:, 0:128], in0=ps[:, 0:128], scalar=r[:, 0:1], in1=kvg_bc[:, :],
                    op0=ALU.mult, op1=ALU.mult), reads=[b_ps, b_r, b_kvg_bc], writes=[b_ck])
                if seq["is_prompt"]:
                    P.op("dve", lambda e: e.tensor_copy(out=ck[:, 128:192], in_=ps[:, 128:192]),
                         reads=[b_ps], writes=[b_ck])
                    row = seq["out_row0"] + t * 128
                    P.dma("sp", nckv_d[row:row + 128, :], ck[:, 0:128], reads=[b_ck], is_output=True)
                    P.dma("sp", nkr_d[row:row + 128, :], ck[:, 128:192], reads=[b_ck], is_output=True)
                P.op("pool", lambda e: e.tensor_copy(out=ckv_tok[:, chunk * 128:(chunk + 1) * 128], in_=ck[:, 0:128]),
                     reads=[b_ck], writes=[b_k])
                if seq["rope"]:
                    p0 = seq_pos0 = t * 128
                    ta, b_ta = tmpa_r.next()
                    tb, b_tb = tmpb_r.next()
                    P.op("dve", lambda e: e.tensor_tensor(
                        out=ta[0:64, 0:128], in0=ps[0:64, 256:384], in1=cosT[:, p0:p0 + 128], op=ALU.mult),
                        reads=[b_ps, b_cosT], writes=[b_ta])
                    P.op("dve", lambda e: e.tensor_tensor(
                        out=tb[0:64, 0:128], in0=ps[0:64, 384:512], in1=sinT[:, p0:p0 + 128], op=ALU.mult),
                        reads=[b_ps, b_sinT], writes=[b_tb])
                    P.op("pool", lambda e: e.tensor_tensor(
                        out=KrT[0:64, tok0:tok0 + 128], in0=ta[0:64, 0:128], in1=tb[0:64, 0:128], op=ALU.add),
                        reads=[b_ta, b_tb], writes=[b_k])
                else:
                    P.op("dve", lambda e: e.tensor_copy(out=KrT[0:64, tok0:tok0 + 128], in_=ps[0:64, 256:384]),
                         reads=[b_ps], writes=[b_k])

            def s4():
                ck, b_ck = S["ck"]
                pst, b_pst = kps.next()
                P.op("pe", lambda e: e.transpose(pst[:, 0:128], ck[:, 0:128], idf[:, 0:128]),
                     reads=[b_ck, b_idf], writes=[b_pst], cost=0.5)
                P.op("dve", lambda e: e.tensor_copy(out=ckvT[:, tok0:tok0 + 128], in_=pst[:, 0:128]),
                     reads=[b_pst], writes=[b_k])

            return [s0, s1, s2, s3, s4]

        xtiles = {}
        b_sync_g, b_sync_s = Buf("sync_g"), Buf("sync_s")

        def front_stages(seq, blk, par):
            t0 = blk * QB
            cond = seq["cond"]
            Qp, b_Qp = QpT[par]
            Qr, b_Qr = QrT[par]
            ao, b_ao = a_outT[par]
            sg, b_sg = sgbT[par]
            S = {}
            stages = []

            def s_load():
                tl = []
                for j in range(2):
                    xt, b_xt = xin.next()
                    row = seq["row0"] + t0 + j * 128
                    P.dma("sp", xt[:], seq["x"][row:row + 128, :], writes=[b_xt])
                    tl.append((xt, b_xt))
                xtiles[(seq["name"], blk)] = tl
                S["tiles"] = tl
            stages.append(s_load)

            for j in range(2):
                def s_norm(j=j):
                    S["xb%d" % j] = norm_tile(*S["tiles"][j])
                stages.append(s_norm)
            for j in range(2):
                def s_tr(j=j):
                    xb, b_xb = S["xb%d" % j]
                    transpose_tile(xb, b_xb, j, cond)
                stages.append(s_tr)

            def s_cq():
                ps, b_ps = psF.next()
                zfm(ps, b_ps, 0, 1536, 128, last_signal=False)
                zfm(ps, b_ps, QB, 1664, 128)
                sq, b_sq = tmpb_r.next()
                P.op("act", lambda e: e.activation(out=cq_sb[:, :], in_=ps[:, :], func=AF.Copy),
                     reads=[b_ps], writes=[b_cq_sb])
                P.op("act", lambda e: e.activation(out=sq[:, :], in_=ps[:, :], func=AF.Square),
                     reads=[b_ps], writes=[b_sq])
                S["sq"] = (sq, b_sq)
            stages.append(s_cq)

            def s_gelu():
                for j in range(2):
                    ps, b_ps = psF.next()
                    for kc in range(8):
                        o = kc * QB + j * 128
                        P.op("pe", lambda e, kc=kc, o=o, ps=ps: e.matmul(
                            ps[:, :], lhsT=hT[:, o:o + 128], rhs=w_in_b[:, kc * WIN + 512:kc * WIN + 1024],
                            start=(kc == 0), stop=(kc == 7)), reads=[b_w_in] + b_hT2[j], writes=[b_ps],
                            signal=(kc == 7))
                    P.op("dve", lambda e, ps=ps, j=j: e.tensor_copy(
                        out=ug[:, 1024 + j * 512:1024 + (j + 1) * 512], in_=ps[:, :]),
                        reads=[b_ps], writes=[b_gvv])
                for pair in range(2):
                    ps, b_ps = psF.next()
                    zfm(ps, b_ps, 0, (pair * 2) * 128, 128, last_signal=False)
                    zfm(ps, b_ps, QB, (pair * 2 + 1) * 128, 128)
                    P.op("dve", lambda e, ps=ps, pair=pair: e.tensor_copy(
                        out=ug[:, pair * 512:(pair + 1) * 512], in_=ps[:, :]),
                        reads=[b_ps], writes=[b_t_f] + ([b_sync_g] if pair == 1 else []))
                P.op("act", lambda e: e.activation(out=ug[:, :], in_=ug[:, :], func=AF.Gelu_apprx_tanh),
                     reads=[b_t_f, b_gvv], writes=[b_t_f, b_gvv], cost=5.0)
            stages.append(s_gelu)

            def s_vnorm():
                vns = []
                for j in range(2):
                    gv, b_gv = ug[:, 1024 + j * 512:1024 + (j + 1) * 512], b_gvv
                    sqv, b_sqv = tmpa_r.next()
                    P.op("pool", lambda e, gv=gv, sqv=sqv: e.tensor_tensor(out=sqv[:, :], in0=gv, in1=gv,
                                                                            op=ALU.mult),
                         reads=[b_gv], writes=[b_sqv])
                    ss, b_ss = st_r.next()
                    P.op("dve", lambda e, ss=ss, sqv=sqv: e.reduce_sum(
                        out=ss[:, 0:4], in_=sqv[:, :].rearrange("p (h d) -> p h d", d=128), axis=AX.X),
                        reads=[b_sqv], writes=[b_ss])
                    r, b_r = rstd_from_sum(ss[:, 0:4], b_ss, 4, 128.0)
                    vn, b_vn = vn_r.next()
                    for h in range(4):
                        P.op("dve", lambda e, h=h, gv=gv, vn=vn, r=r: e.scalar_tensor_tensor(
                            out=vn[:, h * 128:(h + 1) * 128], in0=gv[:, h * 128:(h + 1) * 128],
                            scalar=r[:, h:h + 1], in1=g_v_bc[:, h * 128:(h + 1) * 128],
                            op0=ALU.mult, op1=ALU.mult), reads=[b_gv, b_r, b_g_v_bc], writes=[b_vn])
                    vns.append((vn, b_vn))
                S["vn"] = vns
                sq, b_sq = S["sq"]
                ps2, b_ps2 = psF.next()
                for c in range(2):
                    P.op("pe", lambda e, c=c: e.matmul(
                        ps2[:, 0:QB], lhsT=ones_f[:, :], rhs=sq[:, c * QB:(c + 1) * QB],
                        start=(c == 0), stop=(c == 1)), reads=[b_ones_f, b_sq], writes=[b_ps2], signal=(c == 1))
                P.op("act", lambda e: e.activation(out=rstd_bc[:, :], in_=ps2[:, 0:QB], func=AF.Ln,
                                                   bias=EPS, scale=1.0 / 256.0),
                     reads=[b_ps2], writes=[b_rstd_bc])
                P.op("act", lambda e: e.activation(out=rstd_bc[:, :], in_=rstd_bc[:, :], func=AF.Exp, scale=-0.5),
                     reads=[b_rstd_bc], writes=[b_rstd_bc])
                for c in range(2):
                    P.op("dve", lambda e, c=c: e.scalar_tensor_tensor(
                        out=cqnT[:, c * QB:(c + 1) * QB], in0=cq_sb[:, c * QB:(c + 1) * QB],
                        scalar=qngT[:, c:c + 1], in1=rstd_bc[:, :], op0=ALU.mult, op1=ALU.mult),
                        reads=[b_cq_sb, b_qngT, b_rstd_bc], writes=[b_cqnT])
            stages.append(s_vnorm)

            def s_silu():
                for grp, col0 in ((0, 1024), (1, 1984)):
                    for pair in range(2):
                        ps, b_ps = psF.next()
                        zfm(ps, b_ps, 0, col0 + (pair * 2) * 128, 128, last_signal=False, extra=[b_sync_g])
                        zfm(ps, b_ps, QB, col0 + (pair * 2 + 1) * 128, 128, extra=[b_sync_g])
                        o = grp * 1024 + pair * 512
                        P.op("dve", lambda e, ps=ps, o=o: e.tensor_copy(out=gs[:, o:o + 512], in_=ps[:, :]),
                             reads=[b_ps], writes=[b_gs] + ([b_sync_s] if (grp == 1 and pair == 1) else []))
                P.op("act", lambda e: e.activation(out=gs[:, :], in_=gs[:, :], func=AF.Silu),
                     reads=[b_gs], writes=[b_gs], cost=5.0)
                P.op("pool", lambda e: e.tensor_tensor(out=t_f, in0=t_f, in1=gs[:, 0:1024], op=ALU.mult),
                     reads=[b_t_f, b_gs], writes=[b_t_f], cost=4.0)
                P.op("dve", lambda e: e.tensor_copy(out=sg[:, :], in_=gs[:, 1024:2048]),
                     reads=[b_gs], writes=[b_sg], cost=1.5)
            stages.append(s_silu)

            for pair in range(2):
                def s_qn(pair=pair):
                    ps, b_ps = psF.next()
                    for hh in range(2):
                        h = pair * 2 + hh
                        for kc2 in range(2):
                            P.op("pe", lambda e, h=h, hh=hh, kc2=kc2: e.matmul(
                                ps[:, hh * QB:(hh + 1) * QB],
                                lhsT=w_uq_b[:, kc2 * WUQ + h * 128:kc2 * WUQ + (h + 1) * 128],
                                rhs=cqnT[:, kc2 * QB:(kc2 + 1) * QB], start=(kc2 == 0), stop=(kc2 == 1)),
                                reads=[b_w_uq, b_cqnT, b_sync_s], writes=[b_ps], signal=(hh == 1 and kc2 == 1))
                    qn, b_qn = qn_r.next()
                    P.op("dve", lambda e: e.tensor_copy(out=qn[:, :], in_=ps[:, :]), reads=[b_ps], writes=[b_qn])
                    S["qn%d" % pair] = (qn, b_qn)
                stages.append(s_qn)

            for j in range(2):
                def s_mix(j=j):
                    vn, b_vn = S["vn"][j]
                    ps, b_ps = psF.next()
                    for h in range(4):
                        P.op("pe", lambda e, h=h: e.matmul(
                            ps[:, h * 128:(h + 1) * 128], lhsT=vn[:, h * 128:(h + 1) * 128],
                            rhs=w_sT_b[:, h * 128:(h + 1) * 128], start=True, stop=False),
                            reads=[b_vn, b_w_sT], writes=[b_ps], signal=False)
                        P.op("pe", lambda e, h=h: e.matmul(
                            ps[:, h * 128:(h + 1) * 128], lhsT=ones_b[0:1, 0:128],
                            rhs=b_s_b[0:1, h * 128:(h + 1) * 128], start=False, stop=True),
                            reads=[b_ones_b, b_b_s], writes=[b_ps], signal=(h == 3))
                    a3 = ao[:, :].rearrange("p (h t) -> p h t", h=4)[:, :, j * 128:(j + 1) * 128]
                    t3 = t_f.rearrange("p (h t) -> p h t", h=4)[:, :, j * 128:(j + 1) * 128]
                    p3 = ps[:, :].rearrange("p (h t) -> p h t", h=4)
                    P.op("dve", lambda e: e.tensor_tensor(out=a3, in0=p3, in1=t3, op=ALU.mult),
                         reads=[b_ps, b_t_f], writes=[b_ao])
                stages.append(s_mix)

            for pair in range(2):
                def s_qr(pair=pair):
                    psd, b_psd = psF.next()
                    for hh in range(2):
                        h = pair * 2 + hh
                        for kc2 in range(2):
                            P.op("pe", lambda e, h=h, hh=hh, kc2=kc2: e.matmul(
                                psd[:, hh * QB:(hh + 1) * QB],
                                lhsT=w_uq_b[:, kc2 * WUQ + 512 + h * 128:kc2 * WUQ + 512 + (h + 1) * 128],
                                rhs=cqnT[:, kc2 * QB:(kc2 + 1) * QB], start=(kc2 == 0), stop=(kc2 == 1)),
                                reads=[b_w_uq, b_cqnT, b_sync_s], writes=[b_psd], signal=(hh == 1 and kc2 == 1))
                    if seq["rope"]:
                        pss, b_pss = psF.next()
                        for hh in range(2):
                            h = pair * 2 + hh
                            for kc2 in range(2):
                                P.op("pe", lambda e, h=h, hh=hh, kc2=kc2: e.matmul(
                                    pss[0:64, hh * QB:(hh + 1) * QB],
                                    lhsT=w_uq_b[:, kc2 * WUQ + 1024 + h * 64:kc2 * WUQ + 1024 + (h + 1) * 64],
                                    rhs=cqnT[:, kc2 * QB:(kc2 + 1) * QB], start=(kc2 == 0), stop=(kc2 == 1)),
                                    reads=[b_w_uq, b_cqnT], writes=[b_pss], signal=(hh == 1 and kc2 == 1))
                        ta, b_ta = tmpa_r.next()
                        tb, b_tb = tmpb_r.next()
                        for hh in range(2):
                            P.op("dve", lambda e, hh=hh: e.tensor_tensor(
                                out=ta[0:64, hh * QB:(hh + 1) * QB], in0=psd[0:64, hh * QB:(hh + 1) * QB],
                                in1=cosT[:, t0:t0 + QB], op=ALU.mult), reads=[b_psd, b_cosT], writes=[b_ta])
                            P.op("dve", lambda e, hh=hh: e.tensor_tensor(
                                out=tb[0:64, hh * QB:(hh + 1) * QB], in0=pss[0:64, hh * QB:(hh + 1) * QB],
                                in1=sinT[:, t0:t0 + QB], op=ALU.mult), reads=[b_pss, b_sinT], writes=[b_tb])
                        P.op("pool", lambda e: e.tensor_tensor(
                            out=Qr[0:64, pair * 512:(pair + 1) * 512], in0=ta[0:64, :], in1=tb[0:64, :], op=ALU.add),
                            reads=[b_ta, b_tb], writes=[b_Qr])
                    else:
                        P.op("dve", lambda e: e.tensor_copy(
                            out=Qr[0:64, pair * 512:(pair + 1) * 512], in_=psd[0:64, :]),
                            reads=[b_psd], writes=[b_Qr])
                    P.op("dve", lambda e: e.tensor_copy(
                        out=Qr[64:128, pair * 512:(pair + 1) * 512], in_=psd[64:128, :]),
                        reads=[b_psd], writes=[b_Qr])
                stages.append(s_qr)

            for pair in range(2):
                def s_abs(pair=pair):
                    qn, b_qn = S["qn%d" % pair]
                    ps2, b_ps2 = psF.next()
                    for hh in range(2):
                        h = pair * 2 + hh
                        P.op("pe", lambda e, h=h, hh=hh: e.matmul(
                            ps2[:, hh * QB:(hh + 1) * QB], lhsT=w_nT_b[:, h * 128:(h + 1) * 128],
                            rhs=qn[:, hh * QB:(hh + 1) * QB], start=True, stop=True),
                            reads=[b_w_nT, b_qn], writes=[b_ps2], signal=(hh == 1))
                    P.op("dve", lambda e: e.tensor_copy(
                        out=Qp[:, pair * 512:(pair + 1) * 512], in_=ps2[:, :]),
                        reads=[b_ps2], writes=[b_Qp])
                stages.append(s_abs)
            return stages

        def attn_stages(seq, blk, par):
            Qp, b_Qp = QpT[par]
            Qr, b_Qr = QrT[par]
            sg, b_sg = sgbT[par]
            mxb, b_mxb = mixB[par]
            kb0 = seq["kblk0"]
            chunks = []
            for c in range(seq["T"] // 128):
                col = kb0 * QB + c * 128
                chunks.append((col, col, b_kblk[col // QB]))
            if seq["has_cache"]:
                for c in range(4):
                    chunks.append((TS + c * 128, (16 + c) * 128, b_kcache))
            units = [(ci, half) for ci in range(len(chunks)) for half in range(2)]
            ps_of = {}
            LOOK = 3

            def emit_S(u):
                ci, half = units[u]
                kcol, _, b_k = chunks[ci]
                ps, b_ps = psS.next()
                ps_of[u] = (ps, b_ps)
                P.op("pe", lambda e: e.matmul(
                    ps[:, :], lhsT=ckvT[:, kcol:kcol + 128], rhs=Qp[:, half * 512:(half + 1) * 512],
                    start=True, stop=False), reads=[b_k, b_Qp], writes=[b_ps], signal=False)
                P.op("pe", lambda e: e.matmul(
                    ps[:, :], lhsT=KrT[:, kcol:kcol + 128], rhs=Qr[:, half * 512:(half + 1) * 512],
                    start=False, stop=True), reads=[b_k, b_Qr], writes=[b_ps], signal=True)

            def emit_PV(u):
                ci, half = units[u]
                _, tcol, b_k = chunks[ci]
                ps, b_ps = ps_of.pop(u)
                pt, b_pt = PT_r.next()
                P.op("act", lambda e: e.activation(out=pt[:, :], in_=ps[:, :], func=AF.Exp),
                     reads=[b_ps], writes=[b_pt])
                po, b_po = psO[half]
                ac, b_ac = acc[half]
                first = (ci == 0)
                last = (ci == len(chunks) - 1)
                P.op("pe", lambda e: e.matmul(
                    po[:, :], lhsT=ckv_tok[:, tcol:tcol + 128], rhs=pt[:, :], start=first, stop=last),
                    reads=[b_k, b_pt], writes=[b_po], signal=True)
                eng = "pool"
                if first:
                    P.op(eng, lambda e: e.tensor_copy(out=ac[:, :], in_=pt[:, :]), reads=[b_pt], writes=[b_ac])
                else:
                    P.op(eng, lambda e: e.tensor_tensor(out=ac[:, :], in0=ac[:, :], in1=pt[:, :], op=ALU.add),
                         reads=[b_pt, b_ac], writes=[b_ac])

            stages = []

            def mk(u):
                def f():
                    if u == 0:
                        for uu in range(min(LOOK, len(units))):
                            emit_S(uu)
                    if u + LOOK < len(units):
                        emit_S(u + LOOK)
                    emit_PV(u)
                return f
            for u in range(len(units)):
                stages.append(mk(u))

            def post_hops(half):
                S2 = {}
                po, b_po = psO[half]
                ac, b_ac = acc[half]

                def h1():
                    P.op("act", lambda e: e.activation(
                        out=o_sb[:, half * 512:(half + 1) * 512], in_=po[:, :], func=AF.Copy),
                        reads=[b_po], writes=[b_o_sb])
                    psr, b_psr = psF.next()
                    S2["psr"] = (psr, b_psr)
                    P.op("pe", lambda e: e.matmul(psr[:, :], lhsT=ones_f[:, :], rhs=ac[:, :], start=True, stop=True),
                         reads=[b_ones_f, b_ac], writes=[b_psr])

                def h2():
                    psr, b_psr = S2["psr"]
                    P.op("act", lambda e: e.activation(out=ac[:, :], in_=psr[:, :], func=AF.Ln),
                         reads=[b_psr], writes=[b_ac])
                    P.op("act", lambda e: e.activation(out=ac[:, :], in_=ac[:, :], func=AF.Exp, scale=-1.0),
                         reads=[b_ac], writes=[b_ac])

                def h3():
                    ps, b_ps = psF.next()
                    S2["ps"] = (ps, b_ps)
                    for hh in range(2):
                        h = half * 2 + hh
                        P.op("pe", lambda e, h=h, hh=hh: e.matmul(
                            ps[:, hh * QB:(hh + 1) * QB], lhsT=w_v_b[:, h * 128:(h + 1) * 128],
                            rhs=o_sb[:, h * QB:(h + 1) * QB], start=True, stop=True),
                            reads=[b_w_v, b_o_sb], writes=[b_ps], signal=(hh == 1))

                def h4():
                    ps, b_ps = S2["ps"]
                    ta, b_ta = tmpa_r.next()
                    P.op("dve", lambda e: e.tensor_tensor(out=ta[:, :], in0=ps[:, :], in1=ac[:, :], op=ALU.mult),
                         reads=[b_ps, b_ac], writes=[b_ta])
                    P.op("pool", lambda e: e.tensor_tensor(
                        out=mxb[:, half * 512:(half + 1) * 512], in0=ta[:, :],
                        in1=sg[:, half * 512:(half + 1) * 512], op=ALU.mult),
                        reads=[b_ta, b_sg], writes=[b_mxb])
                return [h1, h2, h3, h4]

            pa, pb_ = post_hops(0), post_hops(1)
            return stages, [pa[0], pb_[0], pa[1], pb_[1], pa[2], pb_[2], pa[3], pb_[3]]

        def back_stages(seq, blk, par, sync=None):
            cond = seq["cond"]
            extra = [sync] if sync is not None else []
            g, b_g = gate_bc[cond]
            ao, b_ao = a_outT[par]
            mxb, b_mxb = mixB[par]
            S = {}
            stages = []
            for j in range(2):
                for n in range(2):
                    def s_mm(j=j, n=n):
                        if n == 0:
                            S["res%d" % j] = yres.next()
                        res, b_res = S["res%d" % j]
                        ps, b_ps = psF.next()
                        for kc in range(8):
                            src, b_src = (ao, b_ao) if kc < 4 else (mxb, b_mxb)
                            o = (kc % 4) * QB + j * 128
                            P.op("pe", lambda e, kc=kc, o=o, src=src: e.matmul(
                                ps[:, :], lhsT=src[:, o:o + 128],
                                rhs=w_o_b[:, kc * D + n * 512:kc * D + (n + 1) * 512],
                                start=(kc == 0), stop=(kc == 7)),
                                reads=[b_src, b_w_o] + extra, writes=[b_ps], signal=(kc == 7))
                        P.op("dve", lambda e: e.tensor_tensor(
                            out=res[:, n * 512:(n + 1) * 512], in0=ps[:, :], in1=g[:, n * 512:(n + 1) * 512],
                            op=ALU.mult), reads=[b_ps, b_g], writes=[b_res])
                    stages.append(s_mm)

                def s_fin(j=j):
                    res, b_res = S["res%d" % j]
                    xt, b_xt = xin.next()
                    row_x = seq["row0"] + blk * QB + j * 128
                    P.dma("sp", xt[:], seq["x"][row_x:row_x + 128, :], writes=[b_xt])
                    P.op("dve", lambda e: e.tensor_tensor(out=res[:, :], in0=res[:, :], in1=xt[:, :], op=ALU.add),
                         reads=[b_res, b_xt], writes=[b_res], cost=1.5)
                    xb, b_xb = xn.next()
                    ss, b_ss = st_r.next()
                    P.op("act", lambda e: e.activation(out=xb[:], in_=res[:], func=AF.Square, accum_out=ss[:, 0:1]),
                         reads=[b_res], writes=[b_xb, b_ss], cost=1.5)
                    r, b_r = rstd_from_sum(ss[:, 0:1], b_ss, 1, float(D))
                    P.op("dve", lambda e: e.scalar_tensor_tensor(
                        out=res[:, :], in0=res[:, :], scalar=r[:, 0:1], in1=final_g_bc[:, :],
                        op0=ALU.mult, op1=ALU.mult), reads=[b_res, b_r, b_final_g_bc], writes=[b_res])
                    row = seq["out_row0"] + blk * QB + j * 128
                    P.dma("sp", seq["y"][row:row + 128, :], res[:, :], reads=[b_res], is_output=True)
                stages.append(s_fin)
            return stages

        seqP = []
        for i in range(NP_PER_CORE):
            seqP.append(dict(name="P%d" % i, x=xp_d, row0=i * TP, T=TP, cond=1, is_prompt=True, rope=False,
                             has_cache=False, y=yp_d, out_row0=i * TP, kblk0=(TS + PAST) // QB + i))
        seqS = dict(name="S", x=xs_d, row0=0, T=TS, cond=0, is_prompt=False, rope=True,
                    has_cache=True, y=ys_d, out_row0=0, kblk0=0)

        def capture(stage_lists, slack_pe=0.0, slack_other=0.0):
            P.begin()
            for sl in stage_lists:
                for f in sl:
                    f()
            return schedule(P.end(), slack_pe, slack_other)

        def emit_sched(sched):
            for _, g in sched:
                for it in g:
                    P.replay_item(it)

        def emit_with_main(main_pair, sched, unit_us):
            main_stages, post_stages = main_pair
            n = len(main_stages)
            span = sched[-1][0] if sched else 0.0
            main_dur = n * unit_us
            stretch = 1.0
            if span > 0:
                stretch = min(2.0, 0.9 * main_dur / span)
            k = 0
            for u, m in enumerate(main_stages):
                m()
                tnow = (u + 1) * unit_us
                while k < len(sched) and sched[k][0] * stretch <= tnow:
                    for it in sched[k][1]:
                        P.replay_item(it)
                    k += 1
            while k < len(sched):
                for it in sched[k][1]:
                    P.replay_item(it)
                k += 1
            for f in post_stages:
                f()

        UNIT_US = 1.0
        SLACK_PE = 3.0
        SLACK_OTHER = 1.0
        nblk = TS // QB

        def attn_flat(seq, blk, par):
            u, p = attn_stages(seq, blk, par)
            return u + p

        work = [ktile_stages(sq_, t) for sq_ in seqP for t in range(sq_["T"] // 128)]
        work += [[late_setup]]
        work += [front_stages(seqS, 0, 0)]
        work += [ktile_stages(seqS, t) for t in range(TS // 128)]
        emit_sched(capture(work))
        blocks = [(seqS, b) for b in range(nblk)] + [(sq_, 0) for sq_ in seqP]
        for g, (sq_, b) in enumerate(blocks):
            side = []
            fr = []
            if g + 1 < len(blocks):
                nsq, nb = blocks[g + 1]
                fr = front_stages(nsq, nb, (g + 1) % 2)
            NS = 9
            side.append(fr[:NS])
            if g > 0:
                psq, pb = blocks[g - 1]
                side.append(back_stages(psq, pb, (g - 1) % 2, sync=(b_sync_s if fr else None)))
            side.append(fr[NS:])
            emit_with_main(attn_stages(sq_, b, g % 2), capture(side, SLACK_PE, SLACK_OTHER), UNIT_US)
        lsq, lb = blocks[-1]
        emit_sched(capture([back_stages(lsq, lb, (len(blocks) - 1) % 2)]))
        P.finish()
        P.emit()
    return nc


def _rope_tables():
    rows = TS // 64
    r, cc = np.meshgrid(np.arange(rows), np.arange(64), indexing="ij")
    r = r.reshape(-1).astype(np.float32)
    cc = cc.reshape(-1).astype(np.float32)
    inv = (np.float32(10000.0) ** (-np.arange(16, dtype=np.float32) / np.float32(16))).astype(np.float32)
    ang = np.concatenate([r[:, None] * inv, cc[:, None] * inv], axis=-1).astype(np.float32)
    cos = np.cos(ang).astype(np.float32)
    sin = np.sin(ang).astype(np.float32)
    cosT = np.ascontiguousarray(np.concatenate([cos, cos], axis=1).T)
    sinT = np.ascontiguousarray(np.concatenate([sin, sin], axis=1).T)
    return cosT, sinT


_NC_CACHE = {}


def kernel(x_prompt, x_sample, cache_ckv, cache_krope, c, c_ctx, norm_g, w_ada, b_ada,
           w_in, w_s, b_s, g_v, q_norm_g, w_uq, kv_norm_g, w_ukv, w_o, final_g):
    f = lambda a: np.ascontiguousarray(np.asarray(a, dtype=np.float32))
    x_prompt, x_sample, cache_ckv, cache_krope = f(x_prompt), f(x_sample), f(cache_ckv), f(cache_krope)
    c, c_ctx = f(c), f(c_ctx)
    cosT, sinT = _rope_tables()
    b_ada1 = f(b_ada)[0]
    shared = {
        "w_ada": f(w_ada)[0],
        "b_adaT": np.ascontiguousarray(b_ada1[0:2048].reshape(16, 128).T),
        "b_gate": np.ascontiguousarray(b_ada1[2048:3072].reshape(1, D)),
        "norm_gT": np.ascontiguousarray(f(norm_g)[0].reshape(8, 128).T),
        "w_in": f(w_in)[0],
        "w_sT": np.ascontiguousarray(np.transpose(f(w_s)[0], (2, 0, 1)).reshape(128, 512)),
        "b_s": f(b_s)[0].reshape(1, 512),
        "g_v": f(g_v)[0].reshape(1, 512),
        "q_norm_gT": np.ascontiguousarray(f(q_norm_g)[0].reshape(2, 128).T),
        "w_uq": f(w_uq)[0],
        "kv_norm_g": f(kv_norm_g)[0].reshape(1, 128),
        "w_ukv": f(w_ukv)[0],
        "w_o": f(w_o)[0],
        "final_g": f(final_g).reshape(1, D),
        "cosT": cosT,
        "sinT": sinT,
    }
    in_maps = []
    for i in range(NCORES):
        cv = np.stack([c[i], c_ctx], axis=0)
        cvT = np.ascontiguousarray(cv.reshape(2, 8, 128).transpose(2, 1, 0).reshape(128, 16))
        m = dict(shared)
        m["xs"] = x_sample[i]
        m["xp"] = np.ascontiguousarray(x_prompt[2 * i:2 * i + 2].reshape(2 * TP, D))
        m["cckv"] = cache_ckv[i, 0]
        m["ckr"] = cache_krope[i, 0]
        m["cvT"] = cvT
        in_maps.append(m)
    if "nc" not in _NC_CACHE:
        _NC_CACHE["nc"] = build_program()
    nc = _NC_CACHE["nc"]
    res = run_bass_kernel_spmd(nc, in_maps, core_ids=list(range(NCORES)))
    R = res.results
    y_sample = np.stack([np.asarray(R[i]["ys"], dtype=np.float32) for i in range(NCORES)], axis=0)
    y_prompt = np.concatenate([np.asarray(R[i]["yp"], dtype=np.float32).reshape(2, TP, D)
                               for i in range(NCORES)], axis=0)
    new_ckv = np.concatenate([np.asarray(R[i]["nckv"], dtype=np.float32).reshape(2, 1, TP, 128)
                              for i in range(NCORES)], axis=0)
    new_kr = np.concatenate([np.asarray(R[i]["nkr"], dtype=np.float32).reshape(2, 1, TP, 64)
                             for i in range(NCORES)], axis=0)
    return (y_prompt, y_sample, new_ckv, new_kr)
```

```python
import contextlib
import math
import numpy as np
import ml_dtypes
import concourse.bass as bass
import concourse.mybir as mybir
from concourse.bass_utils import run_bass_kernel_spmd

F32 = mybir.dt.float32
BF16 = mybir.dt.bfloat16
AF = mybir.ActivationFunctionType
ALU = mybir.AluOpType
AX = mybir.AxisListType

NCORES = 8
D = 1024
TS = 2048
TP = 256
NP_PER_CORE = 2
PAST = 512
QB = 256
EPS = 1e-6
ATTN_SCALE = 1.0 / math.sqrt(192.0)
WIN = 2560
WUQ = 1280


class Tok:
    __slots__ = ("eng", "sem", "val")

    def __init__(self, eng, sem, val):
        self.eng, self.sem, self.val = eng, sem, val


class Buf:
    def __init__(self, name, excl=False):
        self.name = name
        self.excl = excl
        self.w = None
        self.r = []
        self.dsem = None
        self.dcount = 0


class Prog:
    ENGS = ("pe", "act", "dve", "pool", "sp")

    def __init__(self, nc, stack):
        self.nc = nc
        self.stack = stack
        self.q = {e: [] for e in self.ENGS}
        self.esem = {e: stack.enter_context(nc.semaphore("es_" + e)) for e in self.ENGS}
        self.ecount = {e: 0 for e in self.ENGS}
        self.waited = {e: {} for e in self.ENGS}
        self.pending = {e: None for e in self.ENGS}
        self.out_toks = []
        self.cap = None

    def replay_item(self, it):
        if it["kind"] == "op":
            self.op(it["eng"], it["fn"], it["reads"], it["writes"], it["signal"])
        else:
            self.dma(it["eng"], it["out"], it["in_"], it["reads"], it["writes"], it["is_output"])

    def sbuf(self, name, shape, dt):
        return self.stack.enter_context(self.nc.sbuf_tensor("s_" + name, list(shape), dt))

    def psum(self, name, shape, dt):
        return self.stack.enter_context(self.nc.psum_tensor("p_" + name, list(shape), dt))

    def newsem(self, name):
        return self.stack.enter_context(self.nc.semaphore(name))

    def _wait(self, eng, tok):
        assert tok.val is not None, "dependency on unsignalled op (%s)" % tok.eng
        key = id(tok.sem)
        if self.waited[eng].get(key, 0) >= tok.val:
            return
        self.waited[eng][key] = tok.val
        self.q[eng].append(("wait", tok.sem, tok.val))

    def _deps(self, eng, reads, writes, dma_sem=None):
        need = {}

        def add(t):
            assert t.val is not None, "dependency on unsignalled op (%s)" % t.eng
            k = id(t.sem)
            if k not in need or need[k].val < t.val:
                need[k] = t

        for b in reads:
            if b.w is not None:
                add(b.w)
            if b.excl:
                for t in b.r:
                    if t.eng != eng:
                        add(t)
        for b in writes:
            if b.w is not None and b.w.eng != eng and not (dma_sem is not None and b.w.sem is dma_sem):
                add(b.w)
            for t in b.r:
                if t.eng != eng:
                    add(t)
        for t in need.values():
            self._wait(eng, t)

    def begin(self):
        self.cap = []

    def end(self):
        c, self.cap = self.cap, None
        return c

    def op(self, eng, fn, reads=(), writes=(), signal=True, cost=None):
        if self.cap is not None:
            self.cap.append(dict(kind="op", eng=eng, fn=fn, reads=tuple(reads), writes=tuple(writes),
                                 signal=signal, cost=cost))
            return None
        self._deps(eng, reads, writes)
        if signal:
            self.ecount[eng] += 1
            v = self.ecount[eng]
            p = self.pending[eng]
            if p is not None:
                p.val = v
                self.pending[eng] = None
            tok = Tok(eng, self.esem[eng], v)
        else:
            if self.pending[eng] is None:
                self.pending[eng] = Tok(eng, self.esem[eng], None)
            tok = self.pending[eng]
        for b in reads:
            b.r.append(tok)
        for b in writes:
            b.w = tok
            b.r = []
        self.q[eng].append(("op", fn, self.esem[eng] if signal else None, 1))
        return tok

    def dma(self, eng, out, in_, reads=(), writes=(), is_output=False, cost=None):
        if self.cap is not None:
            self.cap.append(dict(kind="dma", eng=eng, out=out, in_=in_, reads=tuple(reads), writes=tuple(writes),
                                 is_output=is_output, signal=True, cost=cost))
            return None
        sb = (list(writes) + list(reads))[0]
        if sb.dsem is None:
            sb.dsem = self.newsem("ds_" + sb.name)
        self._deps(eng, reads, writes, dma_sem=(sb.dsem if writes else None))
        sb.dcount += 16
        tok = Tok(None, sb.dsem, sb.dcount)
        for b in reads:
            b.r.append(tok)
        for b in writes:
            b.w = tok
            b.r = []
        if is_output:
            self.out_toks.append(tok)
        self.q[eng].append(("op", lambda e: e.dma_start(out=out, in_=in_), sb.dsem, 16))
        return tok

    def finish(self, eng="sp"):
        last = {}
        for t in self.out_toks:
            k = id(t.sem)
            if k not in last or last[k].val < t.val:
                last[k] = t
        for t in last.values():
            self.q[eng].append(("wait", t.sem, t.val))

    def emit(self):
        with self.nc.Block() as block:
            def replay(name):
                def f(e):
                    for it in self.q[name]:
                        if it[0] == "wait":
                            e.wait_ge(it[1], it[2])
                        else:
                            ins = it[1](e)
                            if it[2] is not None:
                                ins.then_inc(it[2], it[3])
                return f
            block.tensor(replay("pe"))
            block.scalar(replay("act"))
            block.vector(replay("dve"))
            block.gpsimd(replay("pool"))
            block.sync(replay("sp"))


class Ring:
    def __init__(self, P, name, n, shape, dt, psum=False):
        self.items = []
        self.held = set()
        self.name = name
        for i in range(n):
            nm = "%s%d" % (name, i)
            t = P.psum(nm, shape, dt) if psum else P.sbuf(nm, shape, dt)
            self.items.append((t, Buf(nm, excl=psum)))
        self.i = 0

    def next(self, hold=False):
        n = len(self.items)
        for _ in range(n):
            idx = self.i % n
            self.i += 1
            if idx not in self.held:
                if hold:
                    self.held.add(idx)
                return self.items[idx]
        raise RuntimeError("ring %s exhausted" % self.name)

    def release(self, item):
        for idx, it in enumerate(self.items):
            if it[0] is item[0]:
                self.held.discard(idx)
                return


DEF_COST = {"pe": 0.16, "act": 0.75, "dve": 0.75, "pool": 1.1, "sp": 2.5}


def schedule(items, slack_pe=0.0, slack_other=0.0):
    groups = []
    cur = None
    for it in items:
        if it["eng"] == "pe" and it["kind"] == "op":
            if cur is None:
                cur = []
                groups.append(cur)
            cur.append(it)
            if it["signal"]:
                cur = None
        else:
            groups.append([it])
    w_end, r_end, eng_free, w_dma = {}, {}, {}, {}
    out = []
    for gi, g in enumerate(groups):
        start = 0.0
        cost = 0.0
        for it in g:
            for b in it["reads"]:
                start = max(start, w_end.get(id(b), 0.0))
                if b.excl:
                    start = max(start, r_end.get(id(b), 0.0))
            for b in it["writes"]:
                if not (it["kind"] == "dma" and w_dma.get(id(b), False)):
                    start = max(start, w_end.get(id(b), 0.0))
                start = max(start, r_end.get(id(b), 0.0))
            c = it["cost"]
            if c is None:
                c = 2.5 if it["kind"] == "dma" else DEF_COST[it["eng"]]
            cost += c
        end = start + cost + (slack_pe if (g[0]["kind"] == "op" and g[0]["eng"] == "pe") else slack_other)
        for it in g:
            for b in it["reads"]:
                r_end[id(b)] = max(r_end.get(id(b), 0.0), end)
            for b in it["writes"]:
                w_end[id(b)] = max(end, w_end.get(id(b), 0.0)) if it["kind"] == "dma" else end
                w_dma[id(b)] = (it["kind"] == "dma")
                r_end[id(b)] = 0.0
        out.append((start, gi, g))
    out.sort(key=lambda t: (t[0], t[1]))
    return [(t[0], t[2]) for t in out]


def build_program():
    nc = bass.Bass("TRN2", target_bir_lowering=False)

    def din(name, shape):
        return nc.dram_tensor(name, list(shape), F32, kind="ExternalInput").ap()

    def dout(name, shape):
        return nc.dram_tensor(name, list(shape), F32, kind="ExternalOutput").ap()

    xs_d = din("xs", [TS, D])
    xp_d = din("xp", [NP_PER_CORE * TP, D])
    cckv_d = din("cckv", [PAST, 128])
    ckr_d = din("ckr", [PAST, 64])
    cvT_d = din("cvT", [128, 16])
    w_ada_d = din("w_ada", [D, 3 * D])
    b_adaT_d = din("b_adaT", [128, 16])
    b_gate_d = din("b_gate", [1, D])
    norm_gT_d = din("norm_gT", [128, 8])
    w_in_d = din("w_in", [D, 2496])
    w_sT_d = din("w_sT", [128, 512])
    b_s_d = din("b_s", [1, 512])
    g_v_d = din("g_v", [1, 512])
    q_norm_gT_d = din("q_norm_gT", [128, 2])
    w_uq_d = din("w_uq", [256, 768])
    kv_norm_g_d = din("kv_norm_g", [1, 128])
    w_ukv_d = din("w_ukv", [128, 1024])
    w_o_d = din("w_o", [D, D])
    final_g_d = din("final_g", [1, D])
    cos_d = din("cosT", [64, TS])
    sin_d = din("sinT", [64, TS])

    ys_d = dout("ys", [TS, D])
    yp_d = dout("yp", [NP_PER_CORE * TP, D])
    nckv_d = dout("nckv", [NP_PER_CORE * TP, 128])
    nkr_d = dout("nkr", [NP_PER_CORE * TP, 64])

    NKEY = TS + PAST + NP_PER_CORE * TP

    with contextlib.ExitStack() as st:
        P = Prog(nc, st)

        def persist(name, shape, dt):
            return P.sbuf(name, shape, dt), Buf(name)

        ident_b, b_ident_b = persist("ident_b", [128, 128], BF16)
        ones_f, b_ones_f = persist("ones_f", [128, 128], F32)
        ones_b, b_ones_b = persist("ones_b", [128, 128], BF16)
        cv_f, b_cv_f = persist("cv_f", [128, 16], F32)
        sc_f, b_sc_f = persist("sc_f", [128, 16], F32)
        sc_b, b_sc_b = persist("sc_b", [128, 16], BF16)
        b_adaT, b_b_adaT = persist("b_adaT", [128, 16], F32)
        norm_gT, b_norm_gT = persist("norm_gT", [128, 8], F32)
        tmp16, b_tmp16 = persist("tmp16", [128, 16], F32)
        g1T, b_g1T = persist("g1T", [128, 16], F32)
        shT, b_shT = persist("shT", [128, 16], F32)
        gate_bc = [persist("gate_bc%d" % j, [128, D], F32) for j in range(2)]
        final_g_bc, b_final_g_bc = persist("final_g_bc", [128, D], F32)
        g_v_bc, b_g_v_bc = persist("g_v_bc", [128, 512], F32)
        kvg_bc, b_kvg_bc = persist("kvg_bc", [128, 128], F32)
        qngT, b_qngT = persist("qngT", [128, 2], F32)
        cosT, b_cosT = persist("cosT_sb", [64, TS], BF16)
        sinT, b_sinT = persist("sinT_sb", [64, TS], BF16)
        w_in_b, b_w_in = persist("w_in_b", [128, 8 * WIN], BF16)
        w_o_b, b_w_o = persist("w_o_b", [128, 8 * D], BF16)
        w_uq_b, b_w_uq = persist("w_uq_b", [128, 2 * WUQ], BF16)
        w_nT_b, b_w_nT = persist("w_nT_b", [128, 512], BF16)
        w_v_b, b_w_v = persist("w_v_b", [128, 512], BF16)
        w_sT_b, b_w_sT = persist("w_sT_b", [128, 512], BF16)
        b_s_b, b_b_s = persist("b_s_b", [1, 512], BF16)
        ckvT = P.sbuf("ckvT", [128, NKEY], BF16)
        ckv_tok = P.sbuf("ckv_tok", [128, NKEY], BF16)
        KrT = P.sbuf("KrT", [128, NKEY], BF16)
        b_kblk = [Buf("kblk%d" % i) for i in range(NKEY // QB)]
        b_kcache = Buf("kcache")
        QpT = [persist("QpT%d" % i, [128, 4 * QB], BF16) for i in range(2)]
        QrT = [persist("QrT%d" % i, [128, 4 * QB], BF16) for i in range(2)]
        a_outT = [persist("a_outT%d" % i, [128, 4 * QB], BF16) for i in range(2)]
        sgbT = [persist("sgbT%d" % i, [128, 4 * QB], BF16) for i in range(2)]
        mixB = [persist("mixB%d" % i, [128, 4 * QB], BF16) for i in range(2)]
        hT = P.sbuf("hT", [128, 8 * QB], BF16)
        b_hT2 = [[Buf("hT0d"), Buf("hT0a")], [Buf("hT1d"), Buf("hT1a")]]
        b_hTh = b_hT2[0] + b_hT2[1]
        ug = P.sbuf("ug", [128, 8 * QB], F32)
        b_t_f, b_gvv = Buf("t_f"), Buf("gvv")
        t_f = ug[:, 0:4 * QB]
        gs, b_gs = persist("gs", [128, 8 * QB], F32)
        cq_sb, b_cq_sb = persist("cq_sb", [128, 2 * QB], F32)
        rstd_bc, b_rstd_bc = persist("rstd_bc", [128, QB], F32)
        cqnT, b_cqnT = persist("cqnT", [128, 2 * QB], BF16)
        o_sb, b_o_sb = persist("o_sb", [128, 4 * QB], BF16)
        acc = [persist("acc%d" % i, [128, 512], F32) for i in range(2)]

        xin = Ring(P, "xin", 4, [128, D], F32)
        xn = Ring(P, "xn", 2, [128, D], BF16)
        yres = Ring(P, "yres", 2, [128, D], F32)
        vn_r = Ring(P, "vn", 2, [128, 512], BF16)
        qn_r = Ring(P, "qn", 2, [128, 512], BF16)
        tmpa_r = Ring(P, "tmpa", 2, [128, 512], F32)
        tmpb_r = Ring(P, "tmpb", 2, [128, 512], F32)
        PT_r = Ring(P, "PT", 8, [128, 512], BF16)
        cko_r = Ring(P, "cko", 2, [128, 192], F32)
        st_r = Ring(P, "st", 8, [128, 4], F32)
        st2_r = Ring(P, "st2", 8, [128, 4], F32)
        st3_r = Ring(P, "st3", 8, [128, 4], F32)

        psF = Ring(P, "psF", 2, [128, 512], F32, psum=True)
        psS = Ring(P, "psS", 4, [128, 512], F32, psum=True)

        def psT_next():
            t, bb = psF.next()
            return t[:, :].bitcast(BF16), bb
        psO = [(P.psum("psO%d" % i, [128, 512], F32), Buf("psO%d" % i, excl=True)) for i in range(2)]

        idf, b_idf = persist("ident_f", [128, 128], F32)
        P.op("pool", lambda e: e.memset(idf[:, 0:128], 0.0), writes=[b_idf])
        P.op("pool", lambda e: e.affine_select(out=idf[:, 0:128], in_=idf[:, 0:128], pattern=[[-1, 128]],
                                               compare_op=ALU.not_equal, fill=1.0, base=0,
                                               channel_multiplier=1),
             reads=[b_idf], writes=[b_idf])
        P.op("dve", lambda e: e.tensor_copy(out=ident_b[:], in_=idf[:, 0:128]), reads=[b_idf], writes=[b_ident_b])
        P.op("dve", lambda e: e.memset(ones_f[:], 1.0), writes=[b_ones_f])
        P.op("dve", lambda e: e.memset(ones_b[:], 1.0), writes=[b_ones_b])
        b_krt_init = Buf("krt_init")
        P.op("dve", lambda e: e.memset(KrT[:, :], 0.0), writes=[b_krt_init] + b_kblk + [b_kcache])

        P.dma("sp", cv_f[:], cvT_d, writes=[b_cv_f])
        P.dma("sp", b_adaT[:], b_adaT_d, writes=[b_b_adaT])
        P.dma("sp", norm_gT[:], norm_gT_d, writes=[b_norm_gT])
        P.op("act", lambda e: e.activation(out=sc_f[:], in_=cv_f[:], func=AF.Silu), reads=[b_cv_f], writes=[b_sc_f])
        P.op("dve", lambda e: e.tensor_copy(out=sc_b[:], in_=sc_f[:]), reads=[b_sc_f], writes=[b_sc_b])
        for j in range(2):
            for kc in range(8):
                col = kc * 2 + j
                o = (j * 8 + kc) * 128
                P.op("dve", lambda e, o=o, col=col: e.tensor_scalar_mul(
                    out=hT[:, o:o + 128], in0=ones_b[:, :], scalar1=sc_f[:, col:col + 1]),
                    reads=[b_ones_b, b_sc_f], writes=b_hTh)

        wslots = [xn.items[0], xn.items[1], QpT[0], QpT[1], QrT[0], QrT[1], a_outT[0], a_outT[1],
                  sgbT[0], sgbT[1], mixB[0], mixB[1], (o_sb, b_o_sb)]
        wpieces = []
        pieces = [(kc, piece) for kc in range(8) for piece in range(3)]

        def issue_wada(i):
            kc, piece = pieces[i]
            wb, b_wb = wslots[i % len(wslots)]
            P.dma("pool", wb[:, :], w_ada_d[kc * 128:(kc + 1) * 128, piece * 1024:(piece + 1) * 1024],
                  writes=[b_wb])
            wpieces.append((wb, b_wb))

        for i in range(len(wslots)):
            issue_wada(i)
        b_w_in_k = Buf("w_in_k")
        for kc in range(8):
            P.dma("pool", w_in_b[:, kc * WIN + 1792:kc * WIN + 1984], w_in_d[kc * 128:(kc + 1) * 128, 1792:1984],
                  writes=[b_w_in_k])
        P.dma("pool", cosT[:, :], cos_d, writes=[b_cosT])
        P.dma("pool", sinT[:, :], sin_d, writes=[b_sinT])

        pm, b_pm = psF.next()
        pg = [psS.items[0], psS.items[1], psS.items[2], psO[0]]
        first_pm = [True]
        for i, (kc, piece) in enumerate(pieces):
            wb, b_wb = wpieces[i]
            if piece < 2:
                for c in range(8):
                    cc = piece * 8 + c
                    fs = first_pm[0]
                    first_pm[0] = False
                    P.op("pe", lambda e, cc=cc, c=c, kc=kc, wb=wb, fs=fs: e.matmul(
                        pm[:, cc * 2:cc * 2 + 2], lhsT=wb[:, c * 128:(c + 1) * 128],
                        rhs=sc_b[:, kc * 2:kc * 2 + 2], start=fs, stop=(kc == 7),
                        skip_group_check=True),
                        reads=[b_wb, b_sc_b], writes=[b_pm], signal=(c == 7))
            else:
                for j in range(2):
                    for n in range(2):
                        pt, b_pt = pg[j * 2 + n]
                        o = (j * 8 + kc) * 128
                        P.op("pe", lambda e, pt=pt, o=o, n=n, kc=kc, wb=wb: e.matmul(
                            pt[:, :], lhsT=hT[:, o:o + 128], rhs=wb[:, n * 512:(n + 1) * 512],
                            start=(kc == 0), stop=(kc == 7)),
                            reads=[b_wb] + b_hTh, writes=[b_pt], signal=(j == 1 and n == 1))
            nxt = i + len(wslots)
            if nxt < len(pieces):
                issue_wada(nxt)
        for j in range(2):
            pmv = pm[:, 0:32].rearrange("p (c j) -> p c j", j=2)
            P.op("dve", lambda e, j=j, pmv=pmv: e.scalar_tensor_tensor(
                out=tmp16[:, j * 8:(j + 1) * 8], in0=pmv[:, 8:16, j], scalar=1.0, in1=b_adaT[:, 8:16],
                op0=ALU.add, op1=ALU.add), reads=[b_pm, b_b_adaT], writes=[b_tmp16])
            P.op("dve", lambda e, j=j: e.tensor_tensor(
                out=g1T[:, j * 8:(j + 1) * 8], in0=tmp16[:, j * 8:(j + 1) * 8], in1=norm_gT[:, 0:8],
                op=ALU.mult), reads=[b_tmp16, b_norm_gT], writes=[b_g1T])
            P.op("dve", lambda e, j=j, pmv=pmv: e.tensor_tensor(
                out=shT[:, j * 8:(j + 1) * 8], in0=pmv[:, 0:8, j], in1=b_adaT[:, 0:8],
                op=ALU.add), reads=[b_pm, b_b_adaT], writes=[b_shT])
        bg, b_bg = yres.next()
        P.dma("sp", bg[:], b_gate_d.partition_broadcast(128), writes=[b_bg])
        for j in range(2):
            for n in range(2):
                pt, b_pt = pg[j * 2 + n]
                g, b_g = gate_bc[j]
                P.op("dve", lambda e, pt=pt, g=g, n=n: e.tensor_tensor(
                    out=g[:, n * 512:(n + 1) * 512], in0=pt[:, :], in1=bg[:, n * 512:(n + 1) * 512],
                    op=ALU.add), reads=[b_pt, b_bg], writes=[b_g])

        P.dma("sp", kvg_bc[:], kv_norm_g_d.partition_broadcast(128), writes=[b_kvg_bc])
        P.dma("sp", qngT[:], q_norm_gT_d, writes=[b_qngT])
        P.dma("sp", g_v_bc[:], g_v_d.partition_broadcast(128), writes=[b_g_v_bc])
        P.dma("sp", final_g_bc[:], final_g_d.partition_broadcast(128), writes=[b_final_g_bc])

        def late_setup():
            for kc in range(8):
                P.dma("pool", w_in_b[:, kc * WIN:kc * WIN + 1792], w_in_d[kc * 128:(kc + 1) * 128, 0:1792],
                      reads=[b_w_in_k, b_cosT, b_sinT], writes=[b_w_in], cost=70.0)
                P.dma("pool", w_in_b[:, kc * WIN + 1984:kc * WIN + 2496], w_in_d[kc * 128:(kc + 1) * 128, 1984:2496],
                      reads=[b_w_in_k, b_cosT, b_sinT], writes=[b_w_in], cost=70.0)
            P.dma("pool", b_s_b[:], b_s_d, reads=[b_w_in], writes=[b_b_s], cost=5.0)
            P.dma("pool", w_sT_b[:, :], w_sT_d, reads=[b_w_in], writes=[b_w_sT], cost=5.0)
            for kc in range(8):
                o = kc * WIN
                P.op("dve", lambda e, o=o: e.tensor_scalar_mul(
                    out=w_in_b[:, o + 2496:o + 2528], in0=w_in_b[:, o + 1952:o + 1984], scalar1=-1.0),
                    reads=[b_w_in_k], writes=[b_w_in_k])
                P.op("dve", lambda e, o=o: e.tensor_copy(
                    out=w_in_b[:, o + 2528:o + 2560], in_=w_in_b[:, o + 1920:o + 1952]),
                    reads=[b_w_in_k], writes=[b_w_in_k])
            wn, b_wn = vn_r.next()
            ukv3 = w_ukv_d.rearrange("r (h c) -> r h c", c=256)
            P.dma("pool", wn[:, :].rearrange("p (h c) -> p h c", c=128), ukv3[:, :, 0:128], reads=[b_w_in], writes=[b_wn], cost=8.0)
            P.dma("pool", w_v_b[:, :].rearrange("p (h c) -> p h c", c=128), ukv3[:, :, 128:256], reads=[b_w_in], writes=[b_w_v], cost=8.0)
            psT, b_psT = psT_next()
            for h in range(4):
                P.op("pe", lambda e, h=h: e.transpose(psT[:, h * 128:(h + 1) * 128], wn[:, h * 128:(h + 1) * 128],
                                                      ident_b[:]),
                     reads=[b_wn, b_ident_b], writes=[b_psT], signal=(h == 3))
            P.op("dve", lambda e: e.tensor_copy(out=w_nT_b[:], in_=psT[:, 0:512]), reads=[b_psT], writes=[b_w_nT])
            for kc2 in range(2):
                stg, b_stg = xin.next()
                P.dma("sp", stg[:, 0:768], w_uq_d[kc2 * 128:(kc2 + 1) * 128, :], writes=[b_stg], cost=30.0)
                sv = stg[:, 0:768].rearrange("p (h c) -> p h c", c=192)
                base = kc2 * WUQ
                nope = w_uq_b[:, base:base + 512].rearrange("p (h c) -> p h c", c=128)
                rdup = w_uq_b[:, base + 512:base + 1024].rearrange("p (h c) -> p h c", c=128)
                rsw = w_uq_b[:, base + 1024:base + 1280].rearrange("p (h c) -> p h c", c=64)
                S = ATTN_SCALE
                P.op("dve", lambda e, nope=nope, sv=sv: e.tensor_scalar_mul(out=nope, in0=sv[:, :, 0:128], scalar1=S),
                     reads=[b_stg], writes=[b_w_uq])
                P.op("dve", lambda e, rdup=rdup, sv=sv: e.tensor_scalar_mul(out=rdup[:, :, 0:64], in0=sv[:, :, 128:192], scalar1=S),
                     reads=[b_stg], writes=[b_w_uq])
                P.op("dve", lambda e, rdup=rdup, sv=sv: e.tensor_scalar_mul(out=rdup[:, :, 64:128], in0=sv[:, :, 128:192], scalar1=S),
                     reads=[b_stg], writes=[b_w_uq])
                P.op("dve", lambda e, rsw=rsw, sv=sv: e.tensor_scalar_mul(out=rsw[:, :, 0:32], in0=sv[:, :, 160:192], scalar1=-S),
                     reads=[b_stg], writes=[b_w_uq])
                P.op("dve", lambda e, rsw=rsw, sv=sv: e.tensor_scalar_mul(out=rsw[:, :, 32:64], in0=sv[:, :, 128:160], scalar1=S),
                     reads=[b_stg], writes=[b_w_uq])
            P.dma("pool", ckv_tok[:, 16 * 128:20 * 128].rearrange("p (c r) -> p c r", r=128),
                  cckv_d.rearrange("(c p) r -> p c r", p=128), reads=[b_w_in], writes=[b_kcache], cost=10.0)
            psT1, b_psT1 = psT_next()
            for c in range(4):
                P.op("pe", lambda e, c=c: e.transpose(
                    psT1[:, c * 128:(c + 1) * 128], ckv_tok[:, (16 + c) * 128:(17 + c) * 128], ident_b[:]),
                    reads=[b_kcache, b_ident_b], writes=[b_psT1], signal=(c == 3))
            P.op("dve", lambda e: e.tensor_copy(out=ckvT[:, TS:TS + PAST], in_=psT1[:, 0:512]),
                 reads=[b_psT1], writes=[b_kcache])
            kb, b_kb = qn_r.next()
            P.op("dve", lambda e: e.memset(kb[:, :], 0.0), writes=[b_kb])
            P.dma("pool", kb[:, :].rearrange("p (c r) -> p c r", r=128)[:, :, 64:128],
                  ckr_d.rearrange("(c p) r -> p c r", p=128), reads=[b_w_in], writes=[b_kb], cost=10.0)
            psT2, b_psT2 = psT_next()
            for c in range(4):
                P.op("pe", lambda e, c=c: e.transpose(
                    psT2[:, c * 128:(c + 1) * 128], kb[:, c * 128:(c + 1) * 128], ident_b[:]),
                    reads=[b_kb, b_ident_b], writes=[b_psT2], signal=(c == 3))
            P.op("dve", lambda e: e.tensor_copy(out=KrT[64:128, TS:TS + PAST], in_=psT2[64:128, 0:512]),
                 reads=[b_psT2], writes=[b_kcache])
            for kc in range(8):
                P.dma("pool", w_o_b[:, kc * D:(kc + 1) * D], w_o_d[kc * 128:(kc + 1) * 128, :], reads=[b_w_in], writes=[b_w_o], cost=30.0)


        def rstd_from_sum(ss, b_ss, ncol, n_feat):
            l, b_l = st2_r.next()
            r, b_r = st3_r.next()
            P.op("act", lambda e: e.activation(out=l[:, 0:ncol], in_=ss, func=AF.Ln, bias=EPS, scale=1.0 / n_feat),
                 reads=[b_ss], writes=[b_l])
            P.op("act", lambda e: e.activation(out=r[:, 0:ncol], in_=l[:, 0:ncol], func=AF.Exp, scale=-0.5),
                 reads=[b_l], writes=[b_r])
            return r, b_r

        def norm_tile(xt, b_xt):
            xb, b_xb = xn.next()
            ss, b_ss = st_r.next()
            P.op("act", lambda e: e.activation(out=xb[:], in_=xt[:], func=AF.Square, accum_out=ss[:, 0:1]),
                 reads=[b_xt], writes=[b_xb, b_ss])
            r, b_r = rstd_from_sum(ss[:, 0:1], b_ss, 1, float(D))
            P.op("dve", lambda e: e.tensor_scalar_mul(out=xb[:], in0=xt[:], scalar1=r[:, 0:1]),
                 reads=[b_xt, b_r], writes=[b_xb])
            return xb, b_xb

        def transpose_tile(xb, b_xb, j, cond, pst=None, dst=None):
            psT, b_psT = pst if pst is not None else psT_next()
            if dst is not None:
                dt_, db_ = dst
                for kc in range(8):
                    P.op("pe", lambda e, kc=kc: e.transpose(
                        psT[:, kc * 128:(kc + 1) * 128], xb[:, kc * 128:(kc + 1) * 128], ident_b[:]),
                        reads=[b_xb, b_ident_b], writes=[b_psT], signal=(kc == 7))
                for kc in range(8):
                    col = cond * 8 + kc
                    if kc < 4:
                        P.op("dve", lambda e, kc=kc, col=col: e.tensor_scalar(
                            out=dt_[:, kc * 128:(kc + 1) * 128], in0=psT[:, kc * 128:(kc + 1) * 128],
                            scalar1=g1T[:, col:col + 1], scalar2=shT[:, col:col + 1],
                            op0=ALU.mult, op1=ALU.add), reads=[b_psT, b_g1T, b_shT], writes=[db_], cost=0.4)
                    else:
                        P.op("act", lambda e, kc=kc, col=col: e.activation(
                            out=dt_[:, kc * 128:(kc + 1) * 128], in_=psT[:, kc * 128:(kc + 1) * 128],
                            func=AF.Identity, bias=shT[:, col:col + 1], scale=g1T[:, col:col + 1]),
                            reads=[b_psT, b_g1T, b_shT], writes=[db_], cost=0.5)
                return
            for kc in range(8):
                P.op("pe", lambda e, kc=kc: e.transpose(
                    psT[:, kc * 128:(kc + 1) * 128], xb[:, kc * 128:(kc + 1) * 128], ident_b[:]),
                    reads=[b_xb, b_ident_b], writes=[b_psT], signal=(kc == 7))
            for kc in range(8):
                o = kc * QB + j * 128
                col = cond * 8 + kc
                if True:
                    P.op("dve", lambda e, kc=kc, o=o, col=col: e.tensor_scalar(
                        out=hT[:, o:o + 128], in0=psT[:, kc * 128:(kc + 1) * 128],
                        scalar1=g1T[:, col:col + 1], scalar2=shT[:, col:col + 1],
                        op0=ALU.mult, op1=ALU.add), reads=[b_psT, b_g1T, b_shT], writes=[b_hT2[j][0]])
                else:
                    P.op("act", lambda e, kc=kc, o=o, col=col: e.activation(
                        out=hT[:, o:o + 128], in_=psT[:, kc * 128:(kc + 1) * 128], func=AF.Identity,
                        bias=shT[:, col:col + 1], scale=g1T[:, col:col + 1]),
                        reads=[b_psT, b_g1T, b_shT], writes=[b_hT2[j][1]])

        def zfm(ps, b_ps, pcol, col0, ncols, last_signal=True, extra=()):
            for kc in range(8):
                P.op("pe", lambda e, kc=kc: e.matmul(
                    ps[0:ncols, pcol:pcol + QB], lhsT=w_in_b[:, kc * WIN + col0:kc * WIN + col0 + ncols],
                    rhs=hT[:, kc * QB:(kc + 1) * QB], start=(kc == 0), stop=(kc == 7)),
                    reads=[b_w_in] + b_hTh + list(extra), writes=[b_ps], signal=(last_signal and kc == 7))

        class RingOf:
            def __init__(self, items):
                self.items, self.i = list(items), 0

            def next(self):
                it = self.items[self.i % len(self.items)]
                self.i += 1
                return it

        kps = RingOf(list(psS.items) + list(psO))
        khT = RingOf([QpT[1], QrT[1], a_outT[1], sgbT[1], mixB[1]])

        def ktile_stages(seq, t):
            kb0 = seq["kblk0"]
            tok0 = kb0 * QB + t * 128
            chunk = tok0 // 128
            b_k = b_kblk[tok0 // QB]
            j = t % 2
            cond = seq["cond"]
            S = {}

            def s0():
                xt, b_xt = xin.next()
                row = seq["row0"] + t * 128
                P.dma("sp", xt[:], seq["x"][row:row + 128, :], writes=[b_xt])
                S["x"] = (xt, b_xt)

            def s1():
                S["xb"] = norm_tile(*S["x"])

            def s2():
                kt, b_kt = kps.next()
                S["h"] = khT.next()
                transpose_tile(S["xb"][0], S["xb"][1], j, cond, pst=(kt[:, :].bitcast(BF16), b_kt), dst=S["h"])

            def s3():
                ps, b_ps = kps.next()
                hk, b_hk = S["h"]
                for kc in range(8):
                    o = kc * 128
                    P.op("pe", lambda e, kc=kc, o=o: e.matmul(
                        ps[:, 0:192], lhsT=hk[:, o:o + 128], rhs=w_in_b[:, kc * WIN + 1792:kc * WIN + 1984],
                        start=(kc == 0), stop=(kc == 7)), reads=[b_w_in_k, b_hk], writes=[b_ps], signal=False)
                ngrp = 2 if seq["rope"] else 1
                for gi in range(ngrp):
                    col0 = 1920 if gi == 0 else 2496
                    for kc in range(8):
                        o = kc * 128
                        P.op("pe", lambda e, kc=kc, o=o, gi=gi, col0=col0: e.matmul(
                            ps[0:64, 256 + gi * 128:256 + (gi + 1) * 128],
                            lhsT=w_in_b[:, kc * WIN + col0:kc * WIN + col0 + 64], rhs=hk[:, o:o + 128],
                            start=(kc == 0), stop=(kc == 7)), reads=[b_w_in_k, b_hk], writes=[b_ps],
                            signal=(gi == ngrp - 1 and kc == 7))
                ck, b_ck = cko_r.next()
                S["ck"] = (ck, b_ck)
                ss, b_ss = st_r.next()
                jk, b_jk = tmpa_r.next()
                P.op("act", lambda e: e.activation(out=jk[:, 0:128], in_=ps[:, 0:128], func=AF.Square,
                                                   accum_out=ss[:, 0:1]), reads=[b_ps], writes=[b_jk, b_ss])
                r, b_r = rstd_from_sum(ss[:, 0:1], b_ss, 1, 128.0)
                P.op("dve", lambda e: e.scalar_tensor_tensor(
                    out=ck[:, 0:128], in0=ps[:, 0:128], scalar=r[:, 0:1], in1=kvg_bc[:, :],
                    op0=ALU.mult, op1=ALU.mult), reads=[b_ps, b_r, b_kvg_bc], writes=[b_ck])
                if seq["is_prompt"]:
                    P.op("dve", lambda e: e.tensor_copy(out=ck[:, 128:192], in_=ps[:, 128:192]),
                         reads=[b_ps], writes=[b_ck])
                    row = seq["out_row0"] + t * 128
                    P.dma("sp", nckv_d[row:row + 128, :], ck[:, 0:128], reads=[b_ck], is_output=True)
                    P.dma("sp", nkr_d[row:row + 128, :], ck[:, 128:192], reads=[b_ck], is_output=True)
                P.op("pool", lambda e: e.tensor_copy(out=ckv_tok[:, chunk * 128:(chunk + 1) * 128], in_=ck[:, 0:128]),
                     reads=[b_ck], writes=[b_k])
                if seq["rope"]:
                    p0 = seq_pos0 = t * 128
                    ta, b_ta = tmpa_r.next()
                    tb, b_tb = tmpb_r.next()
                    P.op("dve", lambda e: e.tensor_tensor(
                        out=ta[0:64, 0:128], in0=ps[0:64, 256:384], in1=cosT[:, p0:p0 + 128], op=ALU.mult),
                        reads=[b_ps, b_cosT], writes=[b_ta])
                    P.op("dve", lambda e: e.tensor_tensor(
                        out=tb[0:64, 0:128], in0=ps[0:64, 384:512], in1=sinT[:, p0:p0 + 128], op=ALU.mult),
                        reads=[b_ps, b_sinT], writes=[b_tb])
                    P.op("pool", lambda e: e.tensor_tensor(
                        out=KrT[0:64, tok0:tok0 + 128], in0=ta[0:64, 0:128], in1=tb[0:64, 0:128], op=ALU.add),
                        reads=[b_ta, b_tb], writes=[b_k])
                else:
                    P.op("dve", lambda e: e.tensor_copy(out=KrT[0:64, tok0:tok0 + 128], in_=ps[0:64, 256:384]),
                         reads=[b_ps], writes=[b_k])

            def s4():
                ck, b_ck = S["ck"]
                pst, b_pst = kps.next()
                P.op("pe", lambda e: e.transpose(pst[:, 0:128], ck[:, 0:128], idf[:, 0:128]),
                     reads=[b_ck, b_idf], writes=[b_pst], cost=0.5)
                P.op("dve", lambda e: e.tensor_copy(out=ckvT[:, tok0:tok0 + 128], in_=pst[:, 0:128]),
                     reads=[b_pst], writes=[b_k])

            return [s0, s1, s2, s3, s4]

        xtiles = {}
        b_sync_g, b_sync_s = Buf("sync_g"), Buf("sync_s")

        def front_stages(seq, blk, par):
            t0 = blk * QB
            cond = seq["cond"]
            Qp, b_Qp = QpT[par]
            Qr, b_Qr = QrT[par]
            ao, b_ao = a_outT[par]
            sg, b_sg = sgbT[par]
            S = {}
            stages = []

            def s_load():
                tl = []
                for j in range(2):
                    xt, b_xt = xin.next()
                    row = seq["row0"] + t0 + j * 128
                    P.dma("sp", xt[:], seq["x"][row:row + 128, :], writes=[b_xt])
                    tl.append((xt, b_xt))
                xtiles[(seq["name"], blk)] = tl
                S["tiles"] = tl
            stages.append(s_load)

            for j in range(2):
                def s_norm(j=j):
                    S["xb%d" % j] = norm_tile(*S["tiles"][j])
                stages.append(s_norm)
            for j in range(2):
                def s_tr(j=j):
                    xb, b_xb = S["xb%d" % j]
                    transpose_tile(xb, b_xb, j, cond)
                stages.append(s_tr)

            def s_cq():
                ps, b_ps = psF.next()
                zfm(ps, b_ps, 0, 1536, 128, last_signal=False)
                zfm(ps, b_ps, QB, 1664, 128)
                sq, b_sq = tmpb_r.next()
                P.op("act", lambda e: e.activation(out=cq_sb[:, :], in_=ps[:, :], func=AF.Copy),
                     reads=[b_ps], writes=[b_cq_sb])
                P.op("act", lambda e: e.activation(out=sq[:, :], in_=ps[:, :], func=AF.Square),
                     reads=[b_ps], writes=[b_sq])
                S["sq"] = (sq, b_sq)
            stages.append(s_cq)

            def s_gelu():
                for j in range(2):
                    ps, b_ps = psF.next()
                    for kc in range(8):
                        o = kc * QB + j * 128
                        P.op("pe", lambda e, kc=kc, o=o, ps=ps: e.matmul(
                            ps[:, :], lhsT=hT[:, o:o + 128], rhs=w_in_b[:, kc * WIN + 512:kc * WIN + 1024],
                            start=(kc == 0), stop=(kc == 7)), reads=[b_w_in] + b_hT2[j], writes=[b_ps],
                            signal=(kc == 7))
                    P.op("dve", lambda e, ps=ps, j=j: e.tensor_copy(
                        out=ug[:, 1024 + j * 512:1024 + (j + 1) * 512], in_=ps[:, :]),
                        reads=[b_ps], writes=[b_gvv])
                for pair in range(2):
                    ps, b_ps = psF.next()
                    zfm(ps, b_ps, 0, (pair * 2) * 128, 128, last_signal=False)
                    zfm(ps, b_ps, QB, (pair * 2 + 1) * 128, 128)
                    P.op("dve", lambda e, ps=ps, pair=pair: e.tensor_copy(
                        out=ug[:, pair * 512:(pair + 1) * 512], in_=ps[:, :]),
                        reads=[b_ps], writes=[b_t_f] + ([b_sync_g] if pair == 1 else []))
                P.op("act", lambda e: e.activation(out=ug[:, :], in_=ug[:, :], func=AF.Gelu_apprx_tanh),
                     reads=[b_t_f, b_gvv], writes=[b_t_f, b_gvv], cost=5.0)
            stages.append(s_gelu)

            def s_vnorm():
                vns = []
                for j in range(2):
                    gv, b_gv = ug[:, 1024 + j * 512:1024 + (j + 1) * 512], b_gvv
                    sqv, b_sqv = tmpa_r.next()
                    P.op("pool", lambda e, gv=gv, sqv=sqv: e.tensor_tensor(out=sqv[:, :], in0=gv, in1=gv,
                                                                            op=ALU.mult),
                         reads=[b_gv], writes=[b_sqv])
                    ss, b_ss = st_r.next()
                    P.op("dve", lambda e, ss=ss, sqv=sqv: e.reduce_sum(
                        out=ss[:, 0:4], in_=sqv[:, :].rearrange("p (h d) -> p h d", d=128), axis=AX.X),
                        reads=[b_sqv], writes=[b_ss])
                    r, b_r = rstd_from_sum(ss[:, 0:4], b_ss, 4, 128.0)
                    vn, b_vn = vn_r.next()
                    for h in range(4):
                        P.op("dve", lambda e, h=h, gv=gv, vn=vn, r=r: e.scalar_tensor_tensor(
                            out=vn[:, h * 128:(h + 1) * 128], in0=gv[:, h * 128:(h + 1) * 128],
                            scalar=r[:, h:h + 1], in1=g_v_bc[:, h * 128:(h + 1) * 128],
                            op0=ALU.mult, op1=ALU.mult), reads=[b_gv, b_r, b_g_v_bc], writes=[b_vn])
                    vns.append((vn, b_vn))
                S["vn"] = vns
                sq, b_sq = S["sq"]
                ps2, b_ps2 = psF.next()
                for c in range(2):
                    P.op("pe", lambda e, c=c: e.matmul(
                        ps2[:, 0:QB], lhsT=ones_f[:, :], rhs=sq[:, c * QB:(c + 1) * QB],
                        start=(c == 0), stop=(c == 1)), reads=[b_ones_f, b_sq], writes=[b_ps2], signal=(c == 1))
                P.op("act", lambda e: e.activation(out=rstd_bc[:, :], in_=ps2[:, 0:QB], func=AF.Ln,
                                                   bias=EPS, scale=1.0 / 256.0),
                     reads=[b_ps2], writes=[b_rstd_bc])
                P.op("act", lambda e: e.activation(out=rstd_bc[:, :], in_=rstd_bc[:, :], func=AF.Exp, scale=-0.5),
                     reads=[b_rstd_bc], writes=[b_rstd_bc])
                for c in range(2):
                    P.op("dve", lambda e, c=c: e.scalar_tensor_tensor(
                        out=cqnT[:, c * QB:(c + 1) * QB], in0=cq_sb[:, c * QB:(c + 1) * QB],
                        scalar=qngT[:, c:c + 1], in1=rstd_bc[:, :], op0=ALU.mult, op1=ALU.mult),
                        reads=[b_cq_sb, b_qngT, b_rstd_bc], writes=[b_cqnT])
            stages.append(s_vnorm)

            def s_silu():
                for grp, col0 in ((0, 1024), (1, 1984)):
                    for pair in range(2):
                        ps, b_ps = psF.next()
                        zfm(ps, b_ps, 0, col0 + (pair * 2) * 128, 128, last_signal=False, extra=[b_sync_g])
                        zfm(ps, b_ps, QB, col0 + (pair * 2 + 1) * 128, 128, extra=[b_sync_g])
                        o = grp * 1024 + pair * 512
                        P.op("dve", lambda e, ps=ps, o=o: e.tensor_copy(out=gs[:, o:o + 512], in_=ps[:, :]),
                             reads=[b_ps], writes=[b_gs] + ([b_sync_s] if (grp == 1 and pair == 1) else []))
                P.op("act", lambda e: e.activation(out=gs[:, :], in_=gs[:, :], func=AF.Silu),
                     reads=[b_gs], writes=[b_gs], cost=5.0)
                P.op("pool", lambda e: e.tensor_tensor(out=t_f, in0=t_f, in1=gs[:, 0:1024], op=ALU.mult),
                     reads=[b_t_f, b_gs], writes=[b_t_f], cost=4.0)
                P.op("dve", lambda e: e.tensor_copy(out=sg[:, :], in_=gs[:, 1024:2048]),
                     reads=[b_gs], writes=[b_sg], cost=1.5)
            stages.append(s_silu)

            for pair in range(2):
                def s_qn(pair=pair):
                    ps, b_ps = psF.next()
                    for hh in range(2):
                        h = pair * 2 + hh
                        for kc2 in range(2):
                            P.op("pe", lambda e, h=h, hh=hh, kc2=kc2: e.matmul(
                                ps[:, hh * QB:(hh + 1) * QB],
                                lhsT=w_uq_b[:, kc2 * WUQ + h * 128:kc2 * WUQ + (h + 1) * 128],
                                rhs=cqnT[:, kc2 * QB:(kc2 + 1) * QB], start=(kc2 == 0), stop=(kc2 == 1)),
                                reads=[b_w_uq, b_cqnT, b_sync_s], writes=[b_ps], signal=(hh == 1 and kc2 == 1))
                    qn, b_qn = qn_r.next()
                    P.op("dve", lambda e: e.tensor_copy(out=qn[:, :], in_=ps[:, :]), reads=[b_ps], writes=[b_qn])
                    S["qn%d" % pair] = (qn, b_qn)
                stages.append(s_qn)

            for j in range(2):
                def s_mix(j=j):
                    vn, b_vn = S["vn"][j]
                    ps, b_ps = psF.next()
                    P.op("pe", lambda e: e.matmul(
                        ps[:, 0:512], lhsT=ones_b[0:1, 0:128], rhs=b_s_b[0:1, 0:512], start=True, stop=False),
                        reads=[b_ones_b, b_b_s], writes=[b_ps], signal=False)
                    for h in range(4):
                        P.op("pe", lambda e, h=h: e.matmul(
                            ps[:, h * 128:(h + 1) * 128], lhsT=vn[:, h * 128:(h + 1) * 128],
                            rhs=w_sT_b[:, h * 128:(h + 1) * 128], start=False, stop=(h == 3)),
                            reads=[b_vn, b_w_sT], writes=[b_ps], signal=(h == 3))
                    a3 = ao[:, :].rearrange("p (h t) -> p h t", h=4)[:, :, j * 128:(j + 1) * 128]
                    t3 = t_f.rearrange("p (h t) -> p h t", h=4)[:, :, j * 128:(j + 1) * 128]
                    p3 = ps[:, :].rearrange("p (h t) -> p h t", h=4)
                    P.op("dve", lambda e: e.tensor_tensor(out=a3, in0=p3, in1=t3, op=ALU.mult),
                         reads=[b_ps, b_t_f], writes=[b_ao])
                stages.append(s_mix)

            for pair in range(2):
                def s_qr(pair=pair):
                    psd, b_psd = psF.next()
                    for hh in range(2):
                        h = pair * 2 + hh
                        for kc2 in range(2):
                            P.op("pe", lambda e, h=h, hh=hh, kc2=kc2: e.matmul(
                                psd[:, hh * QB:(hh + 1) * QB],
                                lhsT=w_uq_b[:, kc2 * WUQ + 512 + h * 128:kc2 * WUQ + 512 + (h + 1) * 128],
                                rhs=cqnT[:, kc2 * QB:(kc2 + 1) * QB], start=(kc2 == 0), stop=(kc2 == 1)),
                                reads=[b_w_uq, b_cqnT, b_sync_s], writes=[b_psd], signal=(hh == 1 and kc2 == 1))
                    if seq["rope"]:
                        pss, b_pss = psF.next()
                        for hh in range(2):
                            h = pair * 2 + hh
                            for kc2 in range(2):
                                P.op("pe", lambda e, h=h, hh=hh, kc2=kc2: e.matmul(
                                    pss[0:64, hh * QB:(hh + 1) * QB],
                                    lhsT=w_uq_b[:, kc2 * WUQ + 1024 + h * 64:kc2 * WUQ + 1024 + (h + 1) * 64],
                                    rhs=cqnT[:, kc2 * QB:(kc2 + 1) * QB], start=(kc2 == 0), stop=(kc2 == 1)),
                                    reads=[b_w_uq, b_cqnT], writes=[b_pss], signal=(hh == 1 and kc2 == 1))
                        ta, b_ta = tmpa_r.next()
                        tb, b_tb = tmpb_r.next()
                        for hh in range(2):
                            P.op("dve", lambda e, hh=hh: e.tensor_tensor(
                                out=ta[0:64, hh * QB:(hh + 1) * QB], in0=psd[0:64, hh * QB:(hh + 1) * QB],
                                in1=cosT[:, t0:t0 + QB], op=ALU.mult), reads=[b_psd, b_cosT], writes=[b_ta])
                            P.op("dve", lambda e, hh=hh: e.tensor_tensor(
                                out=tb[0:64, hh * QB:(hh + 1) * QB], in0=pss[0:64, hh * QB:(hh + 1) * QB],
                                in1=sinT[:, t0:t0 + QB], op=ALU.mult), reads=[b_pss, b_sinT], writes=[b_tb])
                        P.op("pool", lambda e: e.tensor_tensor(
                            out=Qr[0:64, pair * 512:(pair + 1) * 512], in0=ta[0:64, :], in1=tb[0:64, :], op=ALU.add),
                            reads=[b_ta, b_tb], writes=[b_Qr])
                    else:
                        P.op("dve", lambda e: e.tensor_copy(
                            out=Qr[0:64, pair * 512:(pair + 1) * 512], in_=psd[0:64, :]),
                            reads=[b_psd], writes=[b_Qr])
                    P.op("dve", lambda e: e.tensor_copy(
                        out=Qr[64:128, pair * 512:(pair + 1) * 512], in_=psd[64:128, :]),
                        reads=[b_psd], writes=[b_Qr])
                stages.append(s_qr)

            for pair in range(2):
                def s_abs(pair=pair):
                    qn, b_qn = S["qn%d" % pair]
                    ps2, b_ps2 = psF.next()
                    for hh in range(2):
                        h = pair * 2 + hh
                        P.op("pe", lambda e, h=h, hh=hh: e.matmul(
                            ps2[:, hh * QB:(hh + 1) * QB], lhsT=w_nT_b[:, h * 128:(h + 1) * 128],
                            rhs=qn[:, hh * QB:(hh + 1) * QB], start=True, stop=True),
                            reads=[b_w_nT, b_qn], writes=[b_ps2], signal=(hh == 1))
                    P.op("dve", lambda e: e.tensor_copy(
                        out=Qp[:, pair * 512:(pair + 1) * 512], in_=ps2[:, :]),
                        reads=[b_ps2], writes=[b_Qp])
                stages.append(s_abs)
            return stages

        def attn_stages(seq, blk, par):
            Qp, b_Qp = QpT[par]
            Qr, b_Qr = QrT[par]
            sg, b_sg = sgbT[par]
            mxb, b_mxb = mixB[par]
            kb0 = seq["kblk0"]
            chunks = []
            for c in range(seq["T"] // 128):
                col = kb0 * QB + c * 128
                chunks.append((col, col, b_kblk[col // QB]))
            if seq["has_cache"]:
                for c in range(4):
                    chunks.append((TS + c * 128, (16 + c) * 128, b_kcache))
            units = [(ci, half) for ci in range(len(chunks)) for half in range(2)]
            ps_of = {}
            LOOK = 3

            def emit_S(u):
                ci, half = units[u]
                kcol, _, b_k = chunks[ci]
                ps, b_ps = psS.next()
                ps_of[u] = (ps, b_ps)
                P.op("pe", lambda e: e.matmul(
                    ps[:, :], lhsT=ckvT[:, kcol:kcol + 128], rhs=Qp[:, half * 512:(half + 1) * 512],
                    start=True, stop=False), reads=[b_k, b_Qp], writes=[b_ps], signal=False)
                P.op("pe", lambda e: e.matmul(
                    ps[:, :], lhsT=KrT[:, kcol:kcol + 128], rhs=Qr[:, half * 512:(half + 1) * 512],
                    start=False, stop=True), reads=[b_k, b_Qr], writes=[b_ps], signal=True)

            def emit_PV(u):
                ci, half = units[u]
                _, tcol, b_k = chunks[ci]
                ps, b_ps = ps_of.pop(u)
                pt, b_pt = PT_r.next()
                P.op("act", lambda e: e.activation(out=pt[:, :], in_=ps[:, :], func=AF.Exp),
                     reads=[b_ps], writes=[b_pt])
                po, b_po = psO[half]
                ac, b_ac = acc[half]
                first = (ci == 0)
                last = (ci == len(chunks) - 1)
                P.op("pe", lambda e: e.matmul(
                    po[:, :], lhsT=ckv_tok[:, tcol:tcol + 128], rhs=pt[:, :], start=first, stop=last),
                    reads=[b_k, b_pt], writes=[b_po], signal=True)
                eng = "pool"
                if first:
                    P.op(eng, lambda e: e.tensor_copy(out=ac[:, :], in_=pt[:, :]), reads=[b_pt], writes=[b_ac])
                else:
                    P.op(eng, lambda e: e.tensor_tensor(out=ac[:, :], in0=ac[:, :], in1=pt[:, :], op=ALU.add),
                         reads=[b_pt, b_ac], writes=[b_ac])

            stages = []

            def mk(u):
                def f():
                    if u == 0:
                        for uu in range(min(LOOK, len(units))):
                            emit_S(uu)
                    if u + LOOK < len(units):
                        emit_S(u + LOOK)
                    emit_PV(u)
                return f
            for u in range(len(units)):
                stages.append(mk(u))

            def post_hops(half):
                S2 = {}
                po, b_po = psO[half]
                ac, b_ac = acc[half]

                def h1():
                    P.op("act", lambda e: e.activation(
                        out=o_sb[:, half * 512:(half + 1) * 512], in_=po[:, :], func=AF.Copy),
                        reads=[b_po], writes=[b_o_sb])
                    psr, b_psr = psF.next()
                    S2["psr"] = (psr, b_psr)
                    P.op("pe", lambda e: e.matmul(psr[:, :], lhsT=ones_f[:, :], rhs=ac[:, :], start=True, stop=True),
                         reads=[b_ones_f, b_ac], writes=[b_psr])

                def h2():
                    psr, b_psr = S2["psr"]
                    P.op("act", lambda e: e.activation(out=ac[:, :], in_=psr[:, :], func=AF.Ln),
                         reads=[b_psr], writes=[b_ac])
                    P.op("act", lambda e: e.activation(out=ac[:, :], in_=ac[:, :], func=AF.Exp, scale=-1.0),
                         reads=[b_ac], writes=[b_ac])

                def h3():
                    ps, b_ps = psF.next()
                    S2["ps"] = (ps, b_ps)
                    for hh in range(2):
                        h = half * 2 + hh
                        P.op("pe", lambda e, h=h, hh=hh: e.matmul(
                            ps[:, hh * QB:(hh + 1) * QB], lhsT=w_v_b[:, h * 128:(h + 1) * 128],
                            rhs=o_sb[:, h * QB:(h + 1) * QB], start=True, stop=True),
                            reads=[b_w_v, b_o_sb], writes=[b_ps], signal=(hh == 1))

                def h4():
                    ps, b_ps = S2["ps"]
                    ta, b_ta = tmpa_r.next()
                    P.op("dve", lambda e: e.tensor_tensor(out=ta[:, :], in0=ps[:, :], in1=ac[:, :], op=ALU.mult),
                         reads=[b_ps, b_ac], writes=[b_ta])
                    P.op("pool", lambda e: e.tensor_tensor(
                        out=mxb[:, half * 512:(half + 1) * 512], in0=ta[:, :],
                        in1=sg[:, half * 512:(half + 1) * 512], op=ALU.mult),
                        reads=[b_ta, b_sg], writes=[b_mxb])
                return [h1, h2, h3, h4]

            pa, pb_ = post_hops(0), post_hops(1)
            return stages, [pa[0], pb_[0], pa[1], pb_[1], pa[2], pb_[2], pa[3], pb_[3]]

        def back_stages(seq, blk, par, sync=None):
            cond = seq["cond"]
            extra = [sync] if sync is not None else []
            g, b_g = gate_bc[cond]
            ao, b_ao = a_outT[par]
            mxb, b_mxb = mixB[par]
            S = {}
            stages = []
            for j in range(2):
                for n in range(2):
                    def s_mm(j=j, n=n):
                        if n == 0:
                            S["res%d" % j] = yres.next()
                        res, b_res = S["res%d" % j]
                        ps, b_ps = psF.next()
                        for kc in range(8):
                            src, b_src = (ao, b_ao) if kc < 4 else (mxb, b_mxb)
                            o = (kc % 4) * QB + j * 128
                            P.op("pe", lambda e, kc=kc, o=o, src=src: e.matmul(
                                ps[:, :], lhsT=src[:, o:o + 128],
                                rhs=w_o_b[:, kc * D + n * 512:kc * D + (n + 1) * 512],
                                start=(kc == 0), stop=(kc == 7)),
                                reads=[b_src, b_w_o] + extra, writes=[b_ps], signal=(kc == 7))
                        P.op("dve", lambda e: e.tensor_tensor(
                            out=res[:, n * 512:(n + 1) * 512], in0=ps[:, :], in1=g[:, n * 512:(n + 1) * 512],
                            op=ALU.mult), reads=[b_ps, b_g], writes=[b_res])
                    stages.append(s_mm)

                def s_fin(j=j):
                    res, b_res = S["res%d" % j]
                    xt, b_xt = xin.next()
                    row_x = seq["row0"] + blk * QB + j * 128
                    P.dma("sp", xt[:], seq["x"][row_x:row_x + 128, :], writes=[b_xt])
                    P.op("dve", lambda e: e.tensor_tensor(out=res[:, :], in0=res[:, :], in1=xt[:, :], op=ALU.add),
                         reads=[b_res, b_xt], writes=[b_res], cost=1.5)
                    xb, b_xb = xn.next()
                    ss, b_ss = st_r.next()
                    P.op("act", lambda e: e.activation(out=xb[:], in_=res[:], func=AF.Square, accum_out=ss[:, 0:1]),
                         reads=[b_res], writes=[b_xb, b_ss], cost=1.5)
                    r, b_r = rstd_from_sum(ss[:, 0:1], b_ss, 1, float(D))
                    P.op("dve", lambda e: e.scalar_tensor_tensor(
                        out=res[:, :], in0=res[:, :], scalar=r[:, 0:1], in1=final_g_bc[:, :],
                        op0=ALU.mult, op1=ALU.mult), reads=[b_res, b_r, b_final_g_bc], writes=[b_res])
                    row = seq["out_row0"] + blk * QB + j * 128
                    P.dma("sp", seq["y"][row:row + 128, :], res[:, :], reads=[b_res], is_output=True)
                stages.append(s_fin)
            return stages

        seqP = []
        for i in range(NP_PER_CORE):
            seqP.append(dict(name="P%d" % i, x=xp_d, row0=i * TP, T=TP, cond=1, is_prompt=True, rope=False,
                             has_cache=False, y=yp_d, out_row0=i * TP, kblk0=(TS + PAST) // QB + i))
        seqS = dict(name="S", x=xs_d, row0=0, T=TS, cond=0, is_prompt=False, rope=True,
                    has_cache=True, y=ys_d, out_row0=0, kblk0=0)

        def capture(stage_lists, slack_pe=0.0, slack_other=0.0):
            P.begin()
            for sl in stage_lists:
                for f in sl:
                    f()
            return schedule(P.end(), slack_pe, slack_other)

        def emit_sched(sched):
            for _, g in sched:
                for it in g:
                    P.replay_item(it)

        def emit_with_main(main_pair, sched, unit_us):
            main_stages, post_stages = main_pair
            n = len(main_stages)
            span = sched[-1][0] if sched else 0.0
            main_dur = n * unit_us
            stretch = 1.0
            if span > 0:
                stretch = min(2.0, 0.9 * main_dur / span)
            k = 0
            for u, m in enumerate(main_stages):
                m()
                tnow = (u + 1) * unit_us
                while k < len(sched) and sched[k][0] * stretch <= tnow:
                    for it in sched[k][1]:
                        P.replay_item(it)
                    k += 1
            while k < len(sched):
                for it in sched[k][1]:
                    P.replay_item(it)
                k += 1
            for f in post_stages:
                f()

        UNIT_US = 1.0
        SLACK_PE = 3.0
        SLACK_OTHER = 1.0
        nblk = TS // QB

        def attn_flat(seq, blk, par):
            u, p = attn_stages(seq, blk, par)
            return u + p

        work = [ktile_stages(sq_, t) for sq_ in seqP for t in range(sq_["T"] // 128)]
        work += [[late_setup]]
        work += [front_stages(seqS, 0, 0)]
        work += [ktile_stages(seqS, t) for t in range(TS // 128)]
        emit_sched(capture(work))
        blocks = [(seqS, b) for b in range(nblk)] + [(sq_, 0) for sq_ in seqP]
        for g, (sq_, b) in enumerate(blocks):
            side = []
            fr = []
            if g + 1 < len(blocks):
                nsq, nb = blocks[g + 1]
                fr = front_stages(nsq, nb, (g + 1) % 2)
            NS = 9
            side.append(fr[:NS])
            if g > 0:
                psq, pb = blocks[g - 1]
                side.append(back_stages(psq, pb, (g - 1) % 2, sync=(b_sync_s if fr else None)))
            side.append(fr[NS:])
            emit_with_main(attn_stages(sq_, b, g % 2), capture(side, SLACK_PE, SLACK_OTHER), UNIT_US)
        lsq, lb = blocks[-1]
        emit_sched(capture([back_stages(lsq, lb, (len(blocks) - 1) % 2)]))
        P.finish()
        P.emit()
    return nc


def _rope_tables():
    rows = TS // 64
    r, cc = np.meshgrid(np.arange(rows), np.arange(64), indexing="ij")
    r = r.reshape(-1).astype(np.float32)
    cc = cc.reshape(-1).astype(np.float32)
    inv = (np.float32(10000.0) ** (-np.arange(16, dtype=np.float32) / np.float32(16))).astype(np.float32)
    ang = np.concatenate([r[:, None] * inv, cc[:, None] * inv], axis=-1).astype(np.float32)
    cos = np.cos(ang).astype(np.float32)
    sin = np.sin(ang).astype(np.float32)
    cosT = np.ascontiguousarray(np.concatenate([cos, cos], axis=1).T)
    sinT = np.ascontiguousarray(np.concatenate([sin, sin], axis=1).T)
    return cosT, sinT


_NC_CACHE = {}


def kernel(x_prompt, x_sample, cache_ckv, cache_krope, c, c_ctx, norm_g, w_ada, b_ada,
           w_in, w_s, b_s, g_v, q_norm_g, w_uq, kv_norm_g, w_ukv, w_o, final_g):
    f = lambda a: np.ascontiguousarray(np.asarray(a, dtype=np.float32))
    x_prompt, x_sample, cache_ckv, cache_krope = f(x_prompt), f(x_sample), f(cache_ckv), f(cache_krope)
    c, c_ctx = f(c), f(c_ctx)
    cosT, sinT = _rope_tables()
    b_ada1 = f(b_ada)[0]
    shared = {
        "w_ada": f(w_ada)[0],
        "b_adaT": np.ascontiguousarray(b_ada1[0:2048].reshape(16, 128).T),
        "b_gate": np.ascontiguousarray(b_ada1[2048:3072].reshape(1, D)),
        "norm_gT": np.ascontiguousarray(f(norm_g)[0].reshape(8, 128).T),
        "w_in": f(w_in)[0],
        "w_sT": np.ascontiguousarray(np.transpose(f(w_s)[0], (2, 0, 1)).reshape(128, 512)),
        "b_s": f(b_s)[0].reshape(1, 512),
        "g_v": f(g_v)[0].reshape(1, 512),
        "q_norm_gT": np.ascontiguousarray(f(q_norm_g)[0].reshape(2, 128).T),
        "w_uq": f(w_uq)[0],
        "kv_norm_g": f(kv_norm_g)[0].reshape(1, 128),
        "w_ukv": f(w_ukv)[0],
        "w_o": f(w_o)[0],
        "final_g": f(final_g).reshape(1, D),
        "cosT": cosT,
        "sinT": sinT,
    }
    in_maps = []
    for i in range(NCORES):
        cv = np.stack([c[i], c_ctx], axis=0)
        cvT = np.ascontiguousarray(cv.reshape(2, 8, 128).transpose(2, 1, 0).reshape(128, 16))
        m = dict(shared)
        m["xs"] = x_sample[i]
        m["xp"] = np.ascontiguousarray(x_prompt[2 * i:2 * i + 2].reshape(2 * TP, D))
        m["cckv"] = cache_ckv[i, 0]
        m["ckr"] = cache_krope[i, 0]
        m["cvT"] = cvT
        in_maps.append(m)
    if "nc" not in _NC_CACHE:
        _NC_CACHE["nc"] = build_program()
    nc = _NC_CACHE["nc"]
    res = run_bass_kernel_spmd(nc, in_maps, core_ids=list(range(NCORES)))
    R = res.results
    y_sample = np.stack([np.asarray(R[i]["ys"], dtype=np.float32) for i in range(NCORES)], axis=0)
    y_prompt = np.concatenate([np.asarray(R[i]["yp"], dtype=np.float32).reshape(2, TP, D)
                               for i in range(NCORES)], axis=0)
    new_ckv = np.concatenate([np.asarray(R[i]["nckv"], dtype=np.float32).reshape(2, 1, TP, 128)
                              for i in range(NCORES)], axis=0)
    new_kr = np.concatenate([np.asarray(R[i]["nkr"], dtype=np.float32).reshape(2, 1, TP, 64)
                             for i in range(NCORES)], axis=0)
    return (y_prompt, y_sample, new_ckv, new_kr)
```
